# Optimizing a Trainium2 kernel written in Bass

```python
import jax, jax.numpy as jnp
from jax import lax
import numpy as np

D_MODEL = 1024
BATCH = 4
SEQ = 8192
DEPTH = 1

N_HEADS = 8
HEAD_DIM = 64
N_KV_HEADS = 2
GQA_GROUP = N_HEADS // N_KV_HEADS
ATTN_WIDTH = N_HEADS * HEAD_DIM
WINDOW = 128
BLOCK = 128
CONV_WIDTH = D_MODEL - ATTN_WIDTH
CONV_GROUPS = 8
CONV_K = 3
MIX_WIDTH = ATTN_WIDTH + CONV_WIDTH
KV_WIDTH = N_KV_HEADS * HEAD_DIM
IN_COLS = ATTN_WIDTH + 2 * KV_WIDTH + 3 * CONV_WIDTH
D_FF = 4 * D_MODEL
EPS = 1e-6
NEG_INF = -1e30

kernel_name = "hymba_swa_sink_shortconv_relu2"


def rms_norm(x, g):
    xf = x.astype(jnp.float32)
    y = xf * lax.rsqrt(jnp.mean(xf * xf, axis=-1, keepdims=True) + EPS)
    return (y * g.astype(jnp.float32)).astype(x.dtype)


def swa_sink_attention(q, k, v, sinks):
    b, s = q.shape[0], q.shape[1]
    nb = s // BLOCK
    qb = q.reshape(b, nb, BLOCK, N_KV_HEADS, GQA_GROUP, HEAD_DIM)
    pad = ((0, 0), (BLOCK, 0), (0, 0), (0, 0))
    kp = jnp.pad(k, pad).reshape(b, nb + 1, BLOCK, N_KV_HEADS, HEAD_DIM)
    vp = jnp.pad(v, pad).reshape(b, nb + 1, BLOCK, N_KV_HEADS, HEAD_DIM)
    kw = jnp.concatenate([kp[:, :-1], kp[:, 1:]], axis=2)
    vw = jnp.concatenate([vp[:, :-1], vp[:, 1:]], axis=2)
    scale = HEAD_DIM ** -0.5
    sc = jnp.einsum('bnqhgd,bnkhd->bnhgqk', qb, kw).astype(jnp.float32) * scale
    blk = jnp.arange(nb)[:, None] * BLOCK
    qpos = blk + jnp.arange(BLOCK)[None, :]
    kpos = blk - BLOCK + jnp.arange(2 * BLOCK)[None, :]
    delta = qpos[:, :, None] - kpos[:, None, :]
    mask = (delta >= 0) & (delta < WINDOW) & (kpos[:, None, :] >= 0)
    sc = jnp.where(mask[None, :, None, None], sc, NEG_INF)
    sink = jnp.broadcast_to(sinks.astype(jnp.float32).reshape(1, 1, N_KV_HEADS, GQA_GROUP, 1, 1),
                            sc.shape[:-1] + (1,))
    p = jax.nn.softmax(jnp.concatenate([sc, sink], axis=-1), axis=-1)[..., :-1]
    o = jnp.einsum('bnhgqk,bnkhd->bnqhgd', p.astype(v.dtype), vw)
    return o.reshape(b, s, ATTN_WIDTH)


def short_conv(u, w):
    s = u.shape[1]
    up = jnp.pad(u, ((0, 0), (CONV_K - 1, 0), (0, 0)))
    out = w[0] * up[:, 0:s]
    for i in range(1, CONV_K):
        out = out + w[i] * up[:, i:i + s]
    return out


def setup_inputs(seed: int = 0) -> dict:
    key = jax.random.key(seed)
    ks = jax.random.split(key, 14)
    f32 = jnp.float32
    def nrm(k, shape, scale):
        return jax.random.normal(k, shape, f32) * scale
    return {
        "x": nrm(ks[0], (BATCH, SEQ, D_MODEL), 1.0),
        "attn_norm_g": 1.0 + nrm(ks[1], (DEPTH, D_MODEL), 0.02),
        "w_in": nrm(ks[2], (DEPTH, D_MODEL, IN_COLS), D_MODEL ** -0.5),
        "q_norm_g": 1.0 + nrm(ks[3], (DEPTH, HEAD_DIM), 0.02),
        "k_norm_g": 1.0 + nrm(ks[4], (DEPTH, HEAD_DIM), 0.02),
        "sinks": nrm(ks[5], (DEPTH, N_HEADS), 0.5),
        "conv_w": nrm(ks[6], (DEPTH, CONV_K, CONV_WIDTH), CONV_K ** -0.5),
        "attn_out_g": 1.0 + nrm(ks[7], (DEPTH, ATTN_WIDTH), 0.02),
        "conv_out_g": 1.0 + nrm(ks[8], (DEPTH, CONV_WIDTH), 0.02),
        "w_out": nrm(ks[9], (DEPTH, MIX_WIDTH, D_MODEL), MIX_WIDTH ** -0.5),
        "mlp_norm_g": 1.0 + nrm(ks[10], (DEPTH, D_MODEL), 0.02),
        "w_up": nrm(ks[11], (DEPTH, D_MODEL, D_FF), D_MODEL ** -0.5),
        "w_down": nrm(ks[12], (DEPTH, D_FF, D_MODEL), D_FF ** -0.5),
    }


def reference(x, attn_norm_g, w_in, q_norm_g, k_norm_g, sinks, conv_w,
              attn_out_g, conv_out_g, w_out, mlp_norm_g, w_up, w_down):
    b, s, _ = x.shape
    for l in range(DEPTH):
        h = rms_norm(x, attn_norm_g[l])
        z = h @ w_in[l]
        o0 = ATTN_WIDTH
        o1 = o0 + KV_WIDTH
        o2 = o1 + KV_WIDTH
        o3 = o2 + CONV_WIDTH
        o4 = o3 + CONV_WIDTH
        q = z[..., :o0].reshape(b, s, N_HEADS, HEAD_DIM)
        k = z[..., o0:o1].reshape(b, s, N_KV_HEADS, HEAD_DIM)
        v = z[..., o1:o2].reshape(b, s, N_KV_HEADS, HEAD_DIM)
        gate_b = z[..., o2:o3]
        gate_c = z[..., o3:o4]
        xc = z[..., o4:]
        q = rms_norm(q, q_norm_g[l])
        k = rms_norm(k, k_norm_g[l])
        y_attn = swa_sink_attention(q, k, v, sinks[l])
        y_conv = gate_b * short_conv(gate_c * xc, conv_w[l])
        y = jnp.concatenate([rms_norm(y_attn, attn_out_g[l]),
                             rms_norm(y_conv, conv_out_g[l])], axis=-1)
        x = x + y @ w_out[l]
        hm = rms_norm(x, mlp_norm_g[l])
        x = x + jnp.square(jax.nn.relu(hm @ w_up[l])) @ w_down[l]
    return x
```

```python
import numpy as np
import ml_dtypes
from contextlib import ExitStack
import concourse.bass as bass
import concourse.mybir as mybir
from concourse.bass_utils import run_bass_kernel_spmd

F32 = mybir.dt.float32
BF16 = mybir.dt.bfloat16
AF = mybir.ActivationFunctionType
ALU = mybir.AluOpType
AX = mybir.AxisListType

D = 1024
T = 4096
NTILE = T // 128
NM = T // 512
O1, O2, O3, O4 = 640, 768, 1280, 1792
EPS = 1e-6
import os as _os0
ENGS = ["pe", "act", "dve", "pool", "sp"]
MASK_ENG = _os0.environ.get("KMASK", "dve")
NPT = int(_os0.environ.get("KNPT", "1"))
NX1 = int(_os0.environ.get("KNX1", "2"))
NHB = int(_os0.environ.get("KNHB", "2"))


class Prog:
    XLAT = float(_os0.environ.get('KXLAT', '700'))
    WIN = float(_os0.environ.get('KWIN', '100'))
    USE_BL = int(_os0.environ.get('KBL', '1'))
    SE_ALL = int(_os0.environ.get('KSEALL', '1'))

    def __init__(self, same_engine_raw=True, do_schedule=True):
        self.ops = []
        self.buf = {}
        self.same_engine_raw = same_engine_raw
        self.do_schedule = do_schedule
        self.last_on_eng = {}
        self.last_dma = {}
        self.pending_bar = {}
        self.bar_first = {}
        self.alias = {}

    def add(self, eng, fn, reads=(), writes=(), dma=None, busy=200.0, lat=None, prio=0):
        i = len(self.ops)
        op = dict(eng=eng, fn=fn, deps={}, dma=dma, sig=False, idx=i, busy=float(busy), lat=float(busy if lat is None else lat), tag=getattr(self, 'tag', ''), prio=prio)
        deps = op["deps"]

        def dep(j, kind):
            if j is None or j == i:
                return
            if deps.get(j) != "raw":
                deps[j] = kind
        for r in reads:
            st = self.buf.setdefault(r, dict(w=None, r=[]))
            dep(st["w"], "raw")
        for w in writes:
            st = self.buf.setdefault(w, dict(w=None, r=[]))
            dep(st["w"], "waw")
            for rr in st["r"]:
                dep(rr, "war")
        for r in reads:
            self.buf[r]["r"].append(i)
        for w in writes:
            st = self.buf[w]
            st["w"] = i
            st["r"] = []
        for nm in list(reads) + list(writes):
            if nm in self.alias:
                for a in self.alias.pop(nm):
                    st = self.buf.get(a)
                    if st is not None:
                        dep(st["w"], "waw")
                        for rr in st["r"]:
                            dep(rr, "war")
        if eng in self.pending_bar:
            for j in self.pending_bar.pop(eng):
                dep(j, "bar")
            self.bar_first[eng] = i
        elif eng in self.bar_first:
            dep(self.bar_first[eng], "bar")
        if dma is not None:
            self.last_dma[dma] = i
        else:
            self.last_on_eng[eng] = i
        self.ops.append(op)
        return i

    def barrier(self):
        deps = list(self.last_on_eng.values()) + list(self.last_dma.values())
        for e in ENGS:
            self.pending_bar[e] = list(deps)

    def schedule(self):
        ops = self.ops
        n = len(ops)
        if not self.do_schedule:
            self.order = list(range(n))
            return
        succ = [[] for _ in range(n)]
        nleft = [0] * n
        for op in ops:
            nleft[op["idx"]] = len(op["deps"])
            for j in op["deps"]:
                succ[j].append(op["idx"])
        blevel = [0.0] * n
        for i in range(n - 1, -1, -1):
            b = blevel[i] + ops[i]["lat"]
            ops[i]["blevel"] = b
            for j in ops[i]["deps"]:
                if b > blevel[j]:
                    blevel[j] = b
        W = self.WIN
        tfree = {e: 0.0 for e in ENGS}
        avail = [0.0] * n
        dready = [0.0] * n
        ready = {e: [] for e in ENGS}
        for op in ops:
            if nleft[op["idx"]] == 0:
                ready[op["eng"]].append(op["idx"])
        order = []
        done = 0
        while done < n:
            best = None
            cands = []
            for e in ENGS:
                tf = tfree[e]
                for i in ready[e]:
                    t = dready[i] if dready[i] > tf else tf
                    cands.append((t, i))
                    if best is None or t < best:
                        best = t
            bk = None
            for (t, i) in cands:
                if t <= best + W:
                    key = (ops[i]["prio"], -ops[i]["blevel"] if self.USE_BL else 0, t, i)
                    if bk is None or key < bk:
                        bk = key
                        bt = t
            i = bk[3]
            t = bt
            op = ops[i]
            e = op["eng"]
            ready[e].remove(i)
            tfree[e] = t + op["busy"]
            avail[i] = t + op["lat"]
            op["t_start"] = t
            order.append(i)
            done += 1
            for k in succ[i]:
                nleft[k] -= 1
                if ops[k]["eng"] == e and op["dma"] is None and ops[k]["dma"] is None:
                    if e != "pe" and (ops[k]["deps"][i] == "raw" or self.SE_ALL) and self.same_engine_raw:
                        a = avail[i]
                    else:
                        a = t + op["busy"]
                else:
                    a = avail[i] + self.XLAT
                if a > dready[k]:
                    dready[k] = a
                    ops[k]["crit"] = i
                if nleft[k] == 0:
                    ready[ops[k]["eng"]].append(k)
        self.order = order
        self.est_total = max(avail)

    def finalize(self):
        self.schedule()
        ops = self.ops
        pos = {i: p for p, i in enumerate(self.order)}
        self.dma_counts = {}
        for i in self.order:
            op = ops[i]
            if op["dma"] is not None:
                self.dma_counts[op["dma"]] = self.dma_counts.get(op["dma"], 0) + 16
                op["dma_val"] = self.dma_counts[op["dma"]]
        for op in ops:
            nd = []
            for j, kind in op["deps"].items():
                p = ops[j]
                assert pos[j] < pos[op["idx"]]
                if p["dma"] is None and op["dma"] is None and p["eng"] == op["eng"]:
                    if op["eng"] == "pe":
                        continue
                    if not ((kind == "raw" or self.SE_ALL) and self.same_engine_raw):
                        continue
                nd.append(j)
            latest = {}
            keep = []
            for j in nd:
                p = ops[j]
                if p["dma"] is not None:
                    keep.append(j)
                else:
                    if p["eng"] not in latest or pos[j] > pos[latest[p["eng"]]]:
                        latest[p["eng"]] = j
            keep += list(latest.values())
            op["xdeps"] = keep
            for j in keep:
                ops[j]["sig"] = True
        cnt = {}
        for i in self.order:
            op = ops[i]
            if op["dma"] is None and op["sig"]:
                cnt[op["eng"]] = cnt.get(op["eng"], 0) + 1
                op["val"] = cnt[op["eng"]]

    def emit(self, engname, engobj, sems, final_waits=()):
        waited = {}
        ops = self.ops
        for i in self.order:
            op = ops[i]
            if op["eng"] != engname:
                continue
            need = {}
            for j in op["xdeps"]:
                p = ops[j]
                if p["dma"] is not None:
                    key = ("dma", p["dma"]); val = p["dma_val"]
                else:
                    key = ("eng", p["eng"]); val = p["val"]
                if val > need.get(key, 0):
                    need[key] = val
            for key, val in need.items():
                if waited.get(key, 0) >= val:
                    continue
                engobj.wait_ge(sems[key], val)
                waited[key] = val
            ins = op["fn"](engobj)
            if op["dma"] is not None:
                ins.then_inc(sems[("dma", op["dma"])], 16)
            elif op["sig"]:
                ins.then_inc(sems[("eng", op["eng"])], 1)
        for key in final_waits:
            engobj.wait_ge(sems[key], self.dma_counts[key[1]])


def bc(ap, axis, n):
    l = [list(x) for x in ap.ap]
    l.insert(axis, [0, n])
    return bass.AP(ap.tensor, ap.offset, l)


class Arena:
    def __init__(self, t32, nbytes):
        self.t32 = t32
        self.t16 = t32.bitcast(BF16)
        self.cap = nbytes
        self.off = 0
        self.recs = []

    def alloc(self, dt, n, names=()):
        size = 4 if dt == F32 else 2
        off = (self.off + 63) // 64 * 64
        assert off + n * size <= self.cap, ("arena overflow", off, n * size, self.cap)
        self.off = off + n * size
        self.recs.append((off, off + n * size, list(names)))
        t = self.t32 if dt == F32 else self.t16
        return t[:, off // size: off // size + n]


def build_nc(debug=False):
    nc = bass.Bass("TRN2", target_bir_lowering=False)

    def din(name, shape, dt=F32):
        return nc.dram_tensor(name, shape, dt, kind="ExternalInput").ap()

    xh = din("xh", [T + 128, D])
    w_in = din("w_in", [D, 2304]); w_out = din("w_out", [D, D]); w_up = din("w_up", [D, 4096]); w_down = din("w_down", [4096, D])
    g1 = din("g1", [1, D]); g2 = din("g2", [1, D]); gq = din("gq", [1, 64]); gk = din("gk", [1, 64])
    sinks = din("sinks", [1, 8]); cw = din("cw", [128, 12]); gA = din("gA", [1, 512]); gC = din("gC", [128, 4])
    ident_d = din("ident", [128, 128], BF16); maskg_d = din("maskg", [128, 256], BF16); mask0_d = din("mask0", [128, 256], BF16)
    out = nc.dram_tensor("out", [T, D], F32, kind="ExternalOutput").ap()
    x1s = nc.dram_tensor("x1s", [T, D], F32, kind="ExternalOutput" if debug else "Internal").ap()

    w_in_v = w_in.rearrange("(c p) n -> p c n", p=128)
    w_out_v = w_out.rearrange("(c p) n -> p c n", p=128)
    w_up_v = w_up.rearrange("(c p) n -> p c n", p=128)
    w_down_v = w_down.rearrange("(f p) n -> p f n", p=128)

    P = Prog()
    with ExitStack() as es:
        WA = es.enter_context(nc.sbuf_tensor("WA", [128, 32768], BF16))
        WUPt = es.enter_context(nc.sbuf_tensor("WUP", [128, 32768], BF16))
        ARB = 212863 - 2 * 65536 - 100
        ARB = ARB // 64 * 64
        art = es.enter_context(nc.sbuf_tensor("arena", [128, ARB // 4], F32))
        ps = [es.enter_context(nc.psum_tensor("ps%d" % b, [128, 512], F32)) for b in range(8)]
        ps16 = [p.bitcast(BF16) for p in ps]
        A = Arena(art, ARB)

        Win = WA[:, 0:18432].rearrange("p (c n) -> p c n", c=8)
        Wout = WA[:, 18432:26624].rearrange("p (c n) -> p c n", c=8)
        hT = WA[:, 26624:30720].rearrange("p (c n) -> p c n", c=8)
        qT = [WA[:, 30720 + i * 512: 30720 + (i + 1) * 512].rearrange("p (c n) -> p c n", c=4) for i in range(4)]
        Wd = WA[:, :].rearrange("p (f n) -> p f n", f=32)
        Wup = WUPt[:, :].rearrange("p (c n) -> p c n", c=8)

        ident = A.alloc(BF16, 128)
        ones = A.alloc(BF16, 32)[:, 0:1]
        stat = A.alloc(F32, 256)
        _sc = [0]

        def st(n):
            o = _sc[0]; _sc[0] += n
            assert _sc[0] <= 256
            return stat[:, o:o + n]
        negB = st(1); ones_unused = st(1); esink_h = st(8); esink_s = st(8); sink_t = st(8)
        ss1 = st(4); l1 = st(4); rs1 = st(4)
        ssqk = st(40); lqk = st(40); rqk = st(40)
        den = st(16); rden = st(16)
        ssA = st(4); lA = st(4); rA = st(4)
        lC = st(8); rC = st(8)
        bmax = st(1)
        cwt = st(12); gCt = st(4)
        mark = A.off

        g1t = A.alloc(F32, 1024, ["g1t"])
        xn = [A.alloc(F32, 1024, ["xn%d" % i]) for i in range(2)]
        hb = [A.alloc(BF16, 1024, ["hb%d" % i]) for i in range(NHB)]
        Csb = A.alloc(F32, 512, ["Csb"]); ubuf = A.alloc(F32, 516, ["ubuf"]); acc = A.alloc(F32, 512, ["acc"]); yc = A.alloc(F32, 512, ["yc"])
        sqc = A.alloc(BF16, 512, ["sqc"])
        carry = A.alloc(F32, 8, ["carry%d" % i for i in range(4)]).rearrange("p (c n) -> p c n", c=4)
        sq32 = A.alloc(F32, 640, ["sq32a", "sq32b"]); tmp32 = A.alloc(F32, 640, ["tmp32a", "tmp32b"])
        qkn = [A.alloc(BF16, 768, ["qkn%d" % i]) for i in range(2)]
        gqk = A.alloc(F32, 768, ["gqk"])
        gq_t = A.alloc(F32, 64, ["gq_t"]); gk_t = A.alloc(F32, 64, ["gk_t"]); prod = A.alloc(F32, 64, ["prod"]); prod2 = A.alloc(F32, 64, ["prod2"])
        gAt = A.alloc(F32, 512, ["gAt"])
        maskg = A.alloc(BF16, 256, ["maskg"]); mask0 = A.alloc(BF16, 256, ["mask0"])
        kTr = A.alloc(BF16, 2 * 8 * 128, ["kT%d" % i for i in range(8)]).rearrange("p (k s n) -> p k s n", k=2, s=8)
        vr = A.alloc(BF16, 8 * 2 * 66, ["vr%d" % i for i in range(8)] + ["vr_ones"]).rearrange("p (s k d) -> p s k d", s=8, k=2)
        pT_l = [A.alloc(BF16, 2048, ["pT%d_%d" % (k, i) for i in range(4)]) for k in range(NPT)]
        yattn_l = [A.alloc(F32, 512, ["yattn%d_0" % k, "yattn%d_1" % k]) for k in range(NPT)]
        ya1 = A.alloc(BF16, 512, ["ya"])
        ya = [ya1, ya1]
        yT = A.alloc(BF16, 2048, ["yT%d" % i for i in range(4)]).rearrange("p (c n) -> p c n", c=4)
        ycT = A.alloc(BF16, 2048, ["ycT%d" % i for i in range(4)]).rearrange("p (c n) -> p c n", c=4)
        xr = [A.alloc(F32, 1024, ["xr%d" % i]) for i in range(2)]
        x1t = [A.alloc(F32, 1024, ["x1t%d_0" % i, "x1t%d_1" % i]) for i in range(NX1)]
        recsA = list(A.recs)
        endA = A.off

        import os as _os
        pools = {"conv": [0, 1, 2, 3, 4, 5, 6], "main": [0, 1, 2, 3, 4, 5, 6], "all": [0, 1, 2, 3, 4, 5, 6, 7]}
        if _os.environ.get("KPOOLS"):
            for part in _os.environ["KPOOLS"].split(";"):
                k, v = part.split(":")
                pools[k] = [int(t) for t in v.split(",")]
        if pools["conv"] == pools["main"]:
            bank_shared = True
        else:
            bank_shared = False
        bank_rr = {"conv": 0, "main": 0, "all": 0}

        def nb(pool="main"):
            if pool == "conv" and bank_shared:
                pool = "main"
            lst = pools[pool]
            b = lst[bank_rr[pool] % len(lst)]
            bank_rr[pool] += 1
            return b

        def fsz(ap):
            n = 1
            for d in ap.shape[1:]:
                n *= d
            return n

        def is_ps(ap):
            try:
                return ap.tensor.name.startswith("ps")
            except Exception:
                return False

        def dma(q, out_ap, in_ap, key, reads=(), writes=(), prio=0):
            nbytes = fsz(out_ap) * out_ap.shape[0] * 4
            if q == "sp":
                busy, lat = 120.0, 2500.0 + nbytes / 200.0
                if key.startswith("c_") and key not in ("c_id", "c_g1"):
                    lat = 22000.0
            else:
                busy, lat = 1500.0, 3500.0 + nbytes / 250.0
            P.add(q, lambda e: e.dma_start(out=out_ap, in_=in_ap), reads=reads, writes=writes, dma=key, busy=busy, lat=lat, prio=prio)

        def act(out_ap, in_ap, func, reads, writes, **kw):
            d = 230.0 + 0.83 * fsz(in_ap)
            P.add("act", lambda e: e.activation(out=out_ap, in_=in_ap, func=func, **kw), reads=reads, writes=writes, busy=d, lat=d + 60)

        def vdur(eng, out_ap, ins):
            n = fsz(out_ap)
            if eng == "dve":
                per = 1.04
                if out_ap.dtype == BF16 and all(a.dtype == BF16 for a in ins):
                    per = 0.55
                d = 110.0 + per * n + (70.0 if any(is_ps(a) for a in ins) else 0.0)
            else:
                d = 200.0 + 1.9 * n
            return d

        def copy(eng, out_ap, in_ap, reads, writes):
            if eng == "act":
                act(out_ap, in_ap, AF.Copy, reads, writes)
            else:
                d = vdur(eng, out_ap, [in_ap])
                P.add(eng, lambda e: e.tensor_copy(out=out_ap, in_=in_ap), reads=reads, writes=writes, busy=d, lat=d + 60)

        def tt(eng, out_ap, in0, in1, op, reads, writes):
            d = vdur(eng, out_ap, [in0, in1])
            P.add(eng, lambda e: e.tensor_tensor(out=out_ap, in0=in0, in1=in1, op=op), reads=reads, writes=writes, busy=d, lat=d + 60)

        def ts(eng, out_ap, in0, s1, op0, reads, writes, s2=None, op1=None):
            d = vdur(eng, out_ap, [in0])
            if op1 is None:
                P.add(eng, lambda e: e.tensor_scalar(out=out_ap, in0=in0, scalar1=s1, scalar2=None, op0=op0), reads=reads, writes=writes, busy=d, lat=d + 60)
            else:
                P.add(eng, lambda e: e.tensor_scalar(out=out_ap, in0=in0, scalar1=s1, scalar2=s2, op0=op0, op1=op1), reads=reads, writes=writes, busy=d, lat=d + 60)

        def stt(out_ap, in0, scalar, in1, op0, op1, reads, writes):
            d = vdur("dve", out_ap, [in0, in1])
            P.add("dve", lambda e: e.scalar_tensor_tensor(out=out_ap, in0=in0, scalar=scalar, in1=in1, op0=op0, op1=op1), reads=reads, writes=writes, busy=d, lat=d + 60)

        def mm(out_ap, lhsT, rhs, start, stop, reads, writes, **kw):
            n = fsz(rhs)
            if lhsT.dtype == F32:
                d = 135.0
            elif lhsT.shape[0] == 64:
                d = 200.0
            else:
                d = max(62.0, 16.0 + 0.405 * n)
            P.add("pe", lambda e: e.matmul(out_ap, lhsT=lhsT, rhs=rhs, start=start, stop=stop, **kw), reads=reads, writes=writes, busy=d, lat=d + 250)

        def tr(out_ap, in_ap, reads, writes):
            P.add("pe", lambda e: e.transpose(out=out_ap, in_=in_ap, identity=ident), reads=list(reads) + ["ident"], writes=writes, busy=90.0, lat=340.0)

        def rstd(ss_ap, l_ap, r_ap, n, names):
            act(l_ap, ss_ap, AF.Ln, [names[0]], [names[1]], scale=1.0 / n, bias=EPS)
            act(r_ap, l_ap, AF.Exp, [names[1]], [names[2]], scale=-0.5)

        def wload(dst, src, key, name, reads=()):
            dma("pool", dst, src, key, reads=reads, writes=[name])
        for kc in range(8):
            wload(Win[:, kc, :], w_in[kc * 128:(kc + 1) * 128, :], "wik%d" % kc, "Wik%d" % kc)
        dma("sp", ident, ident_d, "c_id", writes=["ident"], prio=-1)
        dma("sp", g1t, g1.partition_broadcast(128), "c_g1", writes=["g1t"], prio=-1)
        dma("sp", gq_t, gq.partition_broadcast(128), "c_gq", writes=["gq_t"])
        dma("sp", gk_t, gk.partition_broadcast(128), "c_gk", writes=["gk_t"])
        dma("sp", sink_t, sinks.partition_broadcast(128), "c_sk", writes=["sink_t"])
        dma("sp", cwt, cw, "c_cw", writes=["cwt"])
        dma("sp", gCt, gC, "c_gC", writes=["gCt"])
        dma("sp", gAt, gA.partition_broadcast(128), "c_gA", writes=["gAt"])
        dma("sp", maskg, maskg_d, "c_mg", writes=["maskg"])
        dma("sp", mask0, mask0_d, "c_m0", writes=["mask0"])
        wload(Wout[:, :, :], w_out_v[:, :, :], "wo", "Wout")
        for j in range(8):
            wload(Wup[:, :, j * 512:(j + 1) * 512], w_up_v[:, :, j * 512:(j + 1) * 512], "wu%d" % j, "Wup%d" % j, reads=["x1s%d" % (1 + 4 * (j // 2))])

        P.add("pool", lambda e: e.memset(ones, 1.0), writes=["ones"])
        P.add("pool", lambda e: e.memset(vr[:, :, :, 64:65], 1.0), writes=["vr_ones"])
        cw3 = cwt.rearrange("p (c j) -> p c j", c=4)
        gqk3 = gqk.rearrange("p (h d) -> p h d", d=64)
        copy("dve", gqk3[:, 0:8, :], bc(gq_t, 1, 8), ["gq_t"], ["gqk"])
        ts("dve", gqk3[:, 8:12, :], bc(gk_t, 1, 4), 0.125, ALU.mult, ["gk_t"], ["gqk"])
        tt("dve", prod, gq_t, gk_t, ALU.mult, ["gq_t", "gk_t"], ["prod"])
        ts("dve", prod2, prod, -1.0, ALU.mult, ["prod"], ["prod2"])
        tt("dve", prod, prod, prod2, ALU.max, ["prod", "prod2"], ["prod"])
        P.add("dve", lambda e: e.tensor_reduce(out=bmax, in_=prod, axis=AX.X, op=ALU.max), reads=["prod"], writes=["bmax"])
        ts("dve", negB, bmax, -8.0, ALU.mult, ["bmax"], ["negB"])
        act(esink_h, sink_t, AF.Exp, ["sink_t", "negB"], ["esink_h"], bias=negB, scale=1.0)
        copy("dve", esink_s.rearrange("p (k f c) -> p k f c", k=2, f=2), esink_h.rearrange("p (k c f) -> p k f c", k=2, c=2), ["esink_h"], ["esink_s"])

        def tiles_of(m):
            if m < 0:
                return [(0, 0)]
            return [(s, 1 + 4 * m + s) for s in range(4)]

        def NTs(m):
            P.tag = 'NT%d' % m
            for (s, ti) in tiles_of(m):
                sl = ti % 2; q4 = ti % 4; hs = ti % NHB
                col = s * 128
                dma("sp", xn[sl], xh[ti * 128:(ti + 1) * 128, :], "xn%d" % sl, writes=["xn%d" % sl], prio=-1)
                act(hb[hs], xn[sl], AF.Square, ["xn%d" % sl], ["hb%d" % hs, "ss1_%d" % q4], accum_out=ss1[:, q4:q4 + 1])
                rstd(ss1[:, q4:q4 + 1], l1[:, q4:q4 + 1], rs1[:, q4:q4 + 1], D, ("ss1_%d" % q4, "l1_%d" % q4, "rs1_%d" % q4))
                stt(hb[hs], xn[sl], rs1[:, q4:q4 + 1], g1t, ALU.mult, ALU.mult, ["xn%d" % sl, "rs1_%d" % q4, "g1t"], ["hb%d" % hs])
                b = nb()
                for c in range(8):
                    tr(ps16[b][:, c * 128:(c + 1) * 128], hb[hs][:, c * 128:(c + 1) * 128], ["hb%d" % hs], ["ps%d" % b])
                copy("act", hT[:, :, col:col + 128], ps16[b][:, :].rearrange("p (c n) -> p c n", c=8), ["ps%d" % b], ["hT%d" % s])

        def Zs(m):
            P.tag = 'Z%d' % m
            for (s, ti) in tiles_of(m):
                sl = ti % 2; q4 = ti % 4; s8 = ti % 8
                col = s * 128
                bq = nb()
                for kc in range(8):
                    mm(ps[bq][:, :], hT[:, kc, col:col + 128], Win[:, kc, 0:512], kc == 0, kc == 7, ["hT%d" % s, "Wik%d" % kc], ["ps%d" % bq])
                bk = nb()
                for kc in range(8):
                    mm(ps[bk][:, 0:256], hT[:, kc, col:col + 128], Win[:, kc, 512:768], kc == 0, kc == 7, ["hT%d" % s, "Wik%d" % kc], ["ps%d" % bk])
                act(sq32[:, 0:512], ps[bq][:, :], AF.Square, ["ps%d" % bq], ["sq32a"])
                act(sq32[:, 512:640], ps[bk][:, 0:128], AF.Square, ["ps%d" % bk], ["sq32b"])
                sv = ssqk[:, q4 * 10:(q4 + 1) * 10]; lv = lqk[:, q4 * 10:(q4 + 1) * 10]; rv = rqk[:, q4 * 10:(q4 + 1) * 10]
                P.add("dve", lambda e, sv=sv: e.tensor_reduce(out=sv, in_=sq32.rearrange("p (h d) -> p h d", d=64), axis=AX.X, op=ALU.add),
                      reads=["sq32a", "sq32b"], writes=["ssqk%d" % q4], busy=780.0, lat=840.0)
                rstd(sv, lv, rv, 64, ("ssqk%d" % q4, "lqk%d" % q4, "rqk%d" % q4))
                tt("dve", tmp32[:, 0:512].rearrange("p (h d) -> p h d", d=64), ps[bq][:, :].rearrange("p (h d) -> p h d", d=64),
                   bc(rv[:, 0:8], 2, 64), ALU.mult, ["ps%d" % bq, "rqk%d" % q4], ["tmp32a"])
                tt("dve", tmp32[:, 512:640].rearrange("p (h d) -> p h d", d=64), ps[bk][:, 0:128].rearrange("p (h d) -> p h d", d=64),
                   bc(rv[:, 8:10], 2, 64), ALU.mult, ["ps%d" % bk, "rqk%d" % q4], ["tmp32b"])
                qn = qkn[sl]
                tt("pool", qn[:, 0:512], tmp32[:, 0:512], gqk[:, 0:512], ALU.mult, ["tmp32a", "gqk"], ["qkn%d" % sl])
                tt("pool", qn[:, 512:768].rearrange("p (k u d) -> p k u d", k=2, u=2),
                   bc(tmp32[:, 512:640].rearrange("p (k d) -> p k d", k=2), 2, 2),
                   gqk[:, 512:768].rearrange("p (k u d) -> p k u d", k=2, u=2), ALU.mult, ["tmp32b", "gqk"], ["qkn%d" % sl])
                copy("act", vr[:, s8, :, 0:64], ps[bk][:, 128:256].rearrange("p (k d) -> p k d", k=2), ["ps%d" % bk], ["vr%d" % s8])
                bt = nb()
                for c in range(6):
                    tr(ps16[bt][:, c * 128:(c + 1) * 128], qn[:, c * 128:(c + 1) * 128], ["qkn%d" % sl], ["ps%d" % bt])
                copy("dve", qT[s], ps16[bt][:, 0:512].rearrange("p (c n) -> p c n", c=4), ["ps%d" % bt], ["qT%d" % s])
                copy("dve", kTr[:, :, s8, :], ps16[bt][:, 512:768].rearrange("p (k n) -> p k n", k=2), ["ps%d" % bt], ["kT%d" % s8])

        def ATTs(m):
            P.tag = 'ATT%d' % m
            for (s, ti) in tiles_of(m):
                sl = ti % 2; q4 = ti % 4
                col = s * 128
                kts = [(ti - 1) % 8, ti % 8]
                pi = ti % NPT
                pT = pT_l[pi]; yattn = yattn_l[pi]
                pT5 = pT.rearrange("p (j b c q) -> p j b c q", j=2, b=4, c=2)
                pTn = ["pT%d_%d" % (pi, b_) for b_ in range(4)]
                yan = ["yattn%d_%d" % (pi, k_) for k_ in range(2)]
                for kv in range(2):
                    bSs = [nb(), nb()]
                    for j in range(2):
                        for half in range(2):
                            bS = bSs[half]
                            lo = half * 64
                            mm(ps[bS][:, j * 256:(j + 1) * 256].rearrange("p (c n) -> p c n", c=2),
                               kTr[lo:lo + 64, kv, kts[j], :], qT[s][lo:lo + 64, 2 * kv:2 * kv + 2, :], True, True,
                               ["kT%d" % kts[j], "qT%d" % s], ["ps%d" % bS])
                    for half in range(2):
                        b4 = kv * 2 + half
                        bS = bSs[half]
                        act(pT5[:, :, b4, :, :], ps[bS][:, :].rearrange("p (j c q) -> p j c q", j=2, c=2), AF.Exp,
                            ["ps%d" % bS, "negB"], [pTn[b4]], bias=negB, scale=1.0)
                mk = mask0 if ti == 1 else maskg
                mkn = "mask0" if ti == 1 else "maskg"
                pT4 = pT.rearrange("p (j r q) -> p j r q", j=2, r=8)
                for kv in range(2):
                    pv = pT4[:, :, 4 * kv:4 * kv + 4, :]
                    tt(MASK_ENG, pv, pv, bc(mk.rearrange("p (j q) -> p j q", j=2), 2, 4), ALU.mult,
                       [pTn[2 * kv], pTn[2 * kv + 1], mkn], [pTn[2 * kv], pTn[2 * kv + 1]])
                d8 = den[:, sl * 8:(sl + 1) * 8]; r8 = rden[:, sl * 8:(sl + 1) * 8]
                bOs = []
                for kv in range(2):
                    bO = nb(); bOs.append(bO)
                    for half in range(2):
                        for c in range(2):
                            s4 = half * 2 + c; b4 = kv * 2 + half
                            for j in range(2):
                                mm(ps[bO][:, s4 * 128:s4 * 128 + 65], pT5[:, j, b4, c, :], vr[:, kts[j], kv, 0:65], j == 0, j == 1,
                                   [pTn[b4], "vr%d" % kts[j], "vr_ones"], ["ps%d" % bO])
                    o3v = ps[bO][:, :].rearrange("p (s x) -> p s x", s=4)
                    tt("dve", d8[:, kv * 4:kv * 4 + 4], o3v[:, :, 64], esink_s[:, kv * 4:kv * 4 + 4], ALU.add, ["ps%d" % bO, "esink_s"], ["den%d_%d" % (sl, kv)])
                P.add("dve", lambda e, d8=d8, r8=r8: e.reciprocal(out=r8, in_=d8), reads=["den%d_0" % sl, "den%d_1" % sl], writes=["rden%d" % sl])
                for kv in range(2):
                    bO = bOs[kv]
                    tt("dve", yattn[:, kv * 256:(kv + 1) * 256].rearrange("p (c f d) -> p f c d", c=2, f=2),
                       ps[bO][:, :].rearrange("p (f c x) -> p f c x", f=2, c=2)[:, :, :, 0:64],
                       bc(r8[:, kv * 4:kv * 4 + 4].rearrange("p (f c) -> p f c", f=2), 3, 64), ALU.mult,
                       ["ps%d" % bO, "rden%d" % sl], [yan[kv]])
                act(tmp32[:, 0:512], yattn, AF.Square, yan, ["tmp32a", "ssA%d" % q4], accum_out=ssA[:, q4:q4 + 1])
                rstd(ssA[:, q4:q4 + 1], lA[:, q4:q4 + 1], rA[:, q4:q4 + 1], 512, ("ssA%d" % q4, "lA%d" % q4, "rA%d" % q4))
                stt(ya[sl], yattn, rA[:, q4:q4 + 1], gAt, ALU.mult, ALU.mult, yan + ["rA%d" % q4, "gAt"], ["ya"])
                bt = nb()
                for c in range(4):
                    tr(ps16[bt][:, c * 128:(c + 1) * 128], ya[sl][:, c * 128:(c + 1) * 128], ["ya"], ["ps%d" % bt])
                copy("act", yT[:, :, col:col + 128], ps16[bt][:, 0:512].rearrange("p (c n) -> p c n", c=4), ["ps%d" % bt], ["yT%d" % s])

        def CONVs(m):
            P.tag = 'CONV%d' % m
            halo = m < 0
            N = 128 if halo else 512
            hts = ["hT0"] if halo else ["hT0", "hT1", "hT2", "hT3"]
            m2 = m % 2
            for i in range(4):
                def grp(coff, wname):
                    b = nb("conv")
                    for kc in range(8):
                        mm(ps[b][:, 0:N], Win[:, kc, coff + 128 * i: coff + 128 * (i + 1)], hT[:, kc, 0:N], kc == 0, kc == 7, hts + ["Wik%d" % kc], ["ps%d" % b])
                    return b
                bC = grp(O3, "Win2")
                bX = grp(O4, "Win3")
                copy("act", Csb[:, 0:N], ps[bC][:, 0:N], ["ps%d" % bC], ["Csb"])
                if not halo:
                    copy("dve", ubuf[:, 0:2], carry[:, i, :], ["carry%d" % i], ["ubuf"])
                tt("dve", ubuf[:, 2:2 + N], Csb[:, 0:N], ps[bX][:, 0:N], ALU.mult, ["Csb", "ps%d" % bX], ["ubuf"])
                copy("dve", carry[:, i, :], ubuf[:, N:N + 2], ["ubuf"], ["carry%d" % i])
                if halo:
                    continue
                bB = grp(O2, "Win1")
                ts("dve", acc, ubuf[:, 2:514], cw3[:, i, 2:3], ALU.mult, ["ubuf", "cwt"], ["acc"])
                stt(acc, ubuf[:, 1:513], cw3[:, i, 1:2], acc, ALU.mult, ALU.add, ["ubuf", "cwt", "acc"], ["acc"])
                stt(acc, ubuf[:, 0:512], cw3[:, i, 0:1], acc, ALU.mult, ALU.add, ["ubuf", "cwt", "acc"], ["acc"])
                tt("dve", yc, acc, ps[bB][:, :], ALU.mult, ["acc", "ps%d" % bB], ["yc"])
                act(sqc, yc, AF.Square, ["yc"], ["sqc"])
                for s in range(4):
                    mm(ps[7][:, s:s + 1], sqc[:, s * 128:(s + 1) * 128], ones, (i == 0 and s == 0), (i == 3 and s == 3), ["sqc", "ones"], ["ps7"], skip_group_check=True)
                act(ycT[:, i, :], yc, AF.Identity, ["yc", "gCt"], ["ycT%d" % i], scale=gCt[:, i:i + 1])
            if not halo:
                lv = lC[:, m2 * 4:m2 * 4 + 4]; rv = rC[:, m2 * 4:m2 * 4 + 4]
                act(lv, ps[7][:, 0:4], AF.Ln, ["ps7"], ["lC%d" % m2], scale=1.0 / 512, bias=EPS)
                act(rv, lv, AF.Exp, ["lC%d" % m2], ["rC%d" % m2], scale=-0.5)

        def OUTs(m):
            P.tag = 'OUT%d' % m
            m2 = m % 2
            for (s, ti) in tiles_of(m):
                sl = ti % 2
                col = s * 128
                dma("sp", xr[sl], xh[ti * 128:(ti + 1) * 128, :], "xr%d" % sl, writes=["xr%d" % sl])
                for n in range(2):
                    bA = nb()
                    for c in range(4):
                        mm(ps[bA][:, :], yT[:, c, col:col + 128], Wout[:, c, n * 512:(n + 1) * 512], c == 0, c == 3, ["yT%d" % s, "Wout"], ["ps%d" % bA])
                    bCc = nb()
                    for c in range(4):
                        mm(ps[bCc][:, :], ycT[:, c, col:col + 128], Wout[:, 4 + c, n * 512:(n + 1) * 512], c == 0, c == 3, ["ycT%d" % c, "Wout"], ["ps%d" % bCc])
                    x1 = ti % NX1
                    xs = x1t[x1][:, n * 512:(n + 1) * 512]
                    stt(xs, ps[bCc][:, :], rC[:, m2 * 4 + s:m2 * 4 + s + 1], xr[sl][:, n * 512:(n + 1) * 512], ALU.mult, ALU.add,
                        ["ps%d" % bCc, "rC%d" % m2, "xr%d" % sl], ["x1t%d_%d" % (x1, n)])
                    tt("dve", xs, xs, ps[bA][:, :], ALU.add, ["x1t%d_%d" % (x1, n), "ps%d" % bA], ["x1t%d_%d" % (x1, n)])
                x1 = ti % NX1
                dma("sp", x1s[(ti - 1) * 128:ti * 128, :], x1t[x1], "x1st%d" % x1, reads=["x1t%d_0" % x1, "x1t%d_1" % x1], writes=["x1s%d" % ti])

        NTs(-1); Zs(-1); CONVs(-1)
        NTs(0)
        for m in range(NM):
            Zs(m)
            CONVs(m)
            ATTs(m)
            if m + 1 < NM:
                NTs(m + 1)
            OUTs(m)

        A.off = mark
        nA = len(A.recs)
        g2t = A.alloc(F32, 1024, ["g2t"])
        xn2 = [A.alloc(F32, 1024, ["xn2_%d" % i]) for i in range(2)]
        hm = [A.alloc(BF16, 1024, ["hm%d" % i]) for i in range(2)]
        hmT = [None, None]
        hmT[0] = A.alloc(BF16, 4096, ["hmT0_%d" % i for i in range(4)]).rearrange("p (c n) -> p c n", c=8)
        r32 = [A.alloc(F32, 512, ["r32_%d" % i]) for i in range(2)]
        hmT[1] = A.alloc(BF16, 4096, ["hmT1_%d" % i for i in range(4)]).rearrange("p (c n) -> p c n", c=8)
        xr2 = [A.alloc(F32, 1024, ["xr2_%d_0" % i, "xr2_%d_1" % i]) for i in range(2)]
        actT = A.alloc(BF16, 32 * 512, ["actT%d" % i for i in range(32)]).rearrange("p (f n) -> p f n", f=32)
        ss2 = st(4); l2 = st(4); rs2 = st(4)
        for (b0, b1, bn) in A.recs[nA:]:
            for (a0, a1, an) in recsA:
                if a0 < b1 and b0 < a1:
                    for nm in bn:
                        P.alias.setdefault(nm, []).extend(an)
        wa_in = ["Wik%d" % k_ for k_ in range(8)]
        wa_out = ["Wout"]
        wa_sp = ["hT%d" % i for i in range(4)] + ["qT%d" % i for i in range(4)]
        P.alias["Wd0"] = list(wa_in); P.alias["Wd1"] = list(wa_in); P.alias["Wd2"] = wa_in + wa_out; P.alias["Wd3"] = wa_out + wa_sp

        for j in range(4):
            dma("pool", Wd[:, 8 * j:8 * j + 8, :], w_down_v[:, 8 * j:8 * j + 8, :], "wd%d" % j, writes=["Wd%d" % j])
        dma("sp", g2t, g2.partition_broadcast(128), "c_g2", writes=["g2t"])

        def NT2(m):
            P.tag = 'NT2%d' % m
            mb = m % 2
            for (s, ti) in tiles_of(m):
                sl = ti % 2; q4 = ti % 4
                col = s * 128
                dma("sp", xn2[sl], x1s[(ti - 1) * 128:ti * 128, :], "xn2_%d" % sl, reads=["x1s%d" % ti], writes=["xn2_%d" % sl])
                act(hm[sl], xn2[sl], AF.Square, ["xn2_%d" % sl], ["hm%d" % sl, "ss2_%d" % q4], accum_out=ss2[:, q4:q4 + 1])
                rstd(ss2[:, q4:q4 + 1], l2[:, q4:q4 + 1], rs2[:, q4:q4 + 1], D, ("ss2_%d" % q4, "l2_%d" % q4, "rs2_%d" % q4))
                stt(hm[sl], xn2[sl], rs2[:, q4:q4 + 1], g2t, ALU.mult, ALU.mult, ["xn2_%d" % sl, "rs2_%d" % q4, "g2t"], ["hm%d" % sl])
                b = nb("all")
                for c in range(8):
                    tr(ps16[b][:, c * 128:(c + 1) * 128], hm[sl][:, c * 128:(c + 1) * 128], ["hm%d" % sl], ["ps%d" % b])
                copy("act", hmT[mb][:, :, col:col + 128], ps16[b][:, :].rearrange("p (c n) -> p c n", c=8), ["ps%d" % b], ["hmT%d_%d" % (mb, s)])

        def UP(m):
            P.tag = 'UP%d' % m
            mb = m % 2
            hn = ["hmT%d_%d" % (mb, s) for s in range(4)]
            for f in range(32):
                b = nb("all")
                for kc in range(8):
                    mm(ps[b][:, :], Wup[:, kc, f * 128:(f + 1) * 128], hmT[mb][:, kc, :], kc == 0, kc == 7, hn + ["Wup%d" % (f // 4)], ["ps%d" % b])
                r = r32[f % 2]
                act(r, ps[b][:, :], AF.Relu, ["ps%d" % b], ["r32_%d" % (f % 2)])
                tt("dve" if f % 2 == 0 else "pool", actT[:, f, :], r, r, ALU.mult, ["r32_%d" % (f % 2)], ["actT%d" % f])

        def DOWN(m):
            P.tag = 'DOWN%d' % m
            for (s, ti) in tiles_of(m):
                sl = ti % 2
                col = s * 128
                dma("sp", xr2[sl], x1s[(ti - 1) * 128:ti * 128, :], "xr2_%d" % sl, reads=["x1s%d" % ti], writes=["xr2_%d_0" % sl, "xr2_%d_1" % sl])
                for n in range(2):
                    b = nb("all")
                    for f in range(32):
                        mm(ps[b][:, :], actT[:, f, col:col + 128], Wd[:, f, n * 512:(n + 1) * 512], f == 0, f == 31, ["actT%d" % f, "Wd%d" % (f // 8)], ["ps%d" % b])
                    xs = xr2[sl][:, n * 512:(n + 1) * 512]
                    tt("dve", xs, xs, ps[b][:, :], ALU.add, ["xr2_%d_%d" % (sl, n), "ps%d" % b], ["xr2_%d_%d" % (sl, n)])
                dma("sp", out[(ti - 1) * 128:ti * 128, :], xr2[sl], "ost%d" % sl, reads=["xr2_%d_0" % sl, "xr2_%d_1" % sl])

        NT2(0)
        for m in range(NM):
            UP(m)
            if m + 1 < NM:
                NT2(m + 1)
            DOWN(m)

        P.finalize()
        sems = {}
        for k in P.dma_counts:
            sems[("dma", k)] = es.enter_context(nc.semaphore("d_" + k))
        for k in ENGS:
            sems[("eng", k)] = es.enter_context(nc.semaphore("e_" + k))
        with nc.Block() as block:
            @block.sync
            def _(e):
                P.emit("sp", e, sems, final_waits=[("dma", "ost0"), ("dma", "ost1")])

            @block.gpsimd
            def _(e):
                P.emit("pool", e, sems)

            @block.scalar
            def _(e):
                P.emit("act", e, sems)

            @block.vector
            def _(e):
                P.emit("dve", e, sems)

            @block.tensor
            def _(e):
                P.emit("pe", e, sems)
    return nc


_NC_CACHE = {}


def prep_inputs(x, attn_norm_g, w_in, q_norm_g, k_norm_g, sinks, conv_w, attn_out_g, conv_out_g, w_out, mlp_norm_g, w_up, w_down):
    f = lambda a: np.ascontiguousarray(np.asarray(a, dtype=np.float32))
    x = f(x)
    B, S, _ = x.shape
    assert (B, S) == (4, 8192)
    bf = ml_dtypes.bfloat16
    kk = np.arange(128)[:, None]; qq = np.arange(128)[None, :]
    m_prev = (kk > qq).astype(np.float32); m_cur = (kk <= qq).astype(np.float32)
    maskg = np.concatenate([m_prev, m_cur], axis=1).astype(bf)
    mask0_first = np.concatenate([np.zeros_like(m_prev), m_cur], axis=1).astype(bf)
    ident = np.eye(128, dtype=np.float32).astype(bf)
    common = dict(
        w_in=f(w_in[0]), w_out=f(w_out[0]), w_up=f(w_up[0]), w_down=f(w_down[0]),
        g1=f(attn_norm_g[0]).reshape(1, D), g2=f(mlp_norm_g[0]).reshape(1, D),
        gq=f(q_norm_g[0]).reshape(1, 64), gk=f(k_norm_g[0]).reshape(1, 64), sinks=f(sinks[0]).reshape(1, 8),
        cw=np.ascontiguousarray(f(conv_w[0]).reshape(3, 4, 128).transpose(2, 1, 0).reshape(128, 12)),
        gA=f(attn_out_g[0]).reshape(1, 512),
        gC=np.ascontiguousarray(f(conv_out_g[0]).reshape(4, 128).T),
        ident=ident, maskg=maskg,
    )
    in_maps = []
    for c in range(8):
        b, half = c // 2, c % 2
        if half == 0:
            xh = np.concatenate([np.zeros((128, D), np.float32), x[b, 0:T]], axis=0)
            m0 = mask0_first
        else:
            xh = x[b, T - 128:2 * T]
            m0 = maskg
        d = dict(common)
        d["xh"] = np.ascontiguousarray(xh)
        d["mask0"] = m0
        in_maps.append(d)
    return in_maps


def kernel(**inputs):
    in_maps = prep_inputs(**inputs)
    B, S = 4, 8192
    if "nc" not in _NC_CACHE:
        _NC_CACHE["nc"] = build_nc()
    nc = _NC_CACHE["nc"]
    res = run_bass_kernel_spmd(nc, in_maps, core_ids=list(range(8)))
    outp = np.empty((B, S, D), np.float32)
    for c in range(8):
        b, half = c // 2, c % 2
        outp[b, half * T:(half + 1) * T] = res.results[c]["out"]
    return outp
```

```python
import numpy as np
import ml_dtypes
from contextlib import ExitStack
import concourse.bass as bass
import concourse.mybir as mybir
from concourse.bass_utils import run_bass_kernel_spmd

F32 = mybir.dt.float32
BF16 = mybir.dt.bfloat16
AF = mybir.ActivationFunctionType
ALU = mybir.AluOpType
AX = mybir.AxisListType

D = 1024
T = 4096
NTILE = T // 128
NM = T // 512
O1, O2, O3, O4 = 640, 768, 1280, 1792
EPS = 1e-6
import os as _os0
ENGS = ["pe", "act", "dve", "pool", "sp"]
MASK_ENG = _os0.environ.get("KMASK", "dve")
NPT = int(_os0.environ.get("KNPT", "1"))
NX1 = int(_os0.environ.get("KNX1", "2"))
NHB = int(_os0.environ.get("KNHB", "2"))


class Prog:
    XLAT = float(_os0.environ.get('KXLAT', '700'))
    WIN = float(_os0.environ.get('KWIN', '100'))
    USE_BL = int(_os0.environ.get('KBL', '1'))
    SE_ALL = int(_os0.environ.get('KSEALL', '1'))

    def __init__(self, same_engine_raw=True, do_schedule=True):
        self.ops = []
        self.buf = {}
        self.same_engine_raw = same_engine_raw
        self.do_schedule = do_schedule
        self.last_on_eng = {}
        self.last_dma = {}
        self.pending_bar = {}
        self.bar_first = {}
        self.alias = {}

    def add(self, eng, fn, reads=(), writes=(), dma=None, busy=200.0, lat=None, prio=0):
        i = len(self.ops)
        op = dict(eng=eng, fn=fn, deps={}, dma=dma, sig=False, idx=i, busy=float(busy), lat=float(busy if lat is None else lat), tag=getattr(self, 'tag', ''), prio=prio)
        deps = op["deps"]

        def dep(j, kind):
            if j is None or j == i:
                return
            if deps.get(j) != "raw":
                deps[j] = kind
        for r in reads:
            st = self.buf.setdefault(r, dict(w=None, r=[]))
            dep(st["w"], "raw")
        for w in writes:
            st = self.buf.setdefault(w, dict(w=None, r=[]))
            dep(st["w"], "waw")
            for rr in st["r"]:
                dep(rr, "war")
        for r in reads:
            self.buf[r]["r"].append(i)
        for w in writes:
            st = self.buf[w]
            st["w"] = i
            st["r"] = []
        for nm in list(reads) + list(writes):
            if nm in self.alias:
                for a in self.alias.pop(nm):
                    st = self.buf.get(a)
                    if st is not None:
                        dep(st["w"], "waw")
                        for rr in st["r"]:
                            dep(rr, "war")
        if eng in self.pending_bar:
            for j in self.pending_bar.pop(eng):
                dep(j, "bar")
            self.bar_first[eng] = i
        elif eng in self.bar_first:
            dep(self.bar_first[eng], "bar")
        if dma is not None:
            self.last_dma[dma] = i
        else:
            self.last_on_eng[eng] = i
        self.ops.append(op)
        return i

    def barrier(self):
        deps = list(self.last_on_eng.values()) + list(self.last_dma.values())
        for e in ENGS:
            self.pending_bar[e] = list(deps)

    def schedule(self):
        ops = self.ops
        n = len(ops)
        if not self.do_schedule:
            self.order = list(range(n))
            return
        succ = [[] for _ in range(n)]
        nleft = [0] * n
        for op in ops:
            nleft[op["idx"]] = len(op["deps"])
            for j in op["deps"]:
                succ[j].append(op["idx"])
        blevel = [0.0] * n
        for i in range(n - 1, -1, -1):
            b = blevel[i] + ops[i]["lat"]
            ops[i]["blevel"] = b
            for j in ops[i]["deps"]:
                if b > blevel[j]:
                    blevel[j] = b
        W = self.WIN
        tfree = {e: 0.0 for e in ENGS}
        avail = [0.0] * n
        dready = [0.0] * n
        ready = {e: [] for e in ENGS}
        for op in ops:
            if nleft[op["idx"]] == 0:
                ready[op["eng"]].append(op["idx"])
        order = []
        done = 0
        while done < n:
            best = None
            cands = []
            for e in ENGS:
                tf = tfree[e]
                for i in ready[e]:
                    t = dready[i] if dready[i] > tf else tf
                    cands.append((t, i))
                    if best is None or t < best:
                        best = t
            bk = None
            for (t, i) in cands:
                if t <= best + W:
                    key = (ops[i]["prio"], -ops[i]["blevel"] if self.USE_BL else 0, t, i)
                    if bk is None or key < bk:
                        bk = key
                        bt = t
            i = bk[3]
            t = bt
            op = ops[i]
            e = op["eng"]
            ready[e].remove(i)
            tfree[e] = t + op["busy"]
            avail[i] = t + op["lat"]
            op["t_start"] = t
            order.append(i)
            done += 1
            for k in succ[i]:
                nleft[k] -= 1
                if ops[k]["eng"] == e and op["dma"] is None and ops[k]["dma"] is None:
                    if e != "pe" and (ops[k]["deps"][i] == "raw" or self.SE_ALL) and self.same_engine_raw:
                        a = avail[i]
                    else:
                        a = t + op["busy"]
                else:
                    a = avail[i] + self.XLAT
                if a > dready[k]:
                    dready[k] = a
                    ops[k]["crit"] = i
                if nleft[k] == 0:
                    ready[ops[k]["eng"]].append(k)
        self.order = order
        self.est_total = max(avail)

    def finalize(self):
        self.schedule()
        ops = self.ops
        pos = {i: p for p, i in enumerate(self.order)}
        self.dma_counts = {}
        for i in self.order:
            op = ops[i]
            if op["dma"] is not None:
                self.dma_counts[op["dma"]] = self.dma_counts.get(op["dma"], 0) + 16
                op["dma_val"] = self.dma_counts[op["dma"]]
        for op in ops:
            nd = []
            for j, kind in op["deps"].items():
                p = ops[j]
                assert pos[j] < pos[op["idx"]]
                if p["dma"] is None and op["dma"] is None and p["eng"] == op["eng"]:
                    if op["eng"] == "pe":
                        continue
                    if not ((kind == "raw" or self.SE_ALL) and self.same_engine_raw):
                        continue
                nd.append(j)
            latest = {}
            keep = []
            for j in nd:
                p = ops[j]
                if p["dma"] is not None:
                    keep.append(j)
                else:
                    if p["eng"] not in latest or pos[j] > pos[latest[p["eng"]]]:
                        latest[p["eng"]] = j
            keep += list(latest.values())
            op["xdeps"] = keep
            for j in keep:
                ops[j]["sig"] = True
        cnt = {}
        for i in self.order:
            op = ops[i]
            if op["dma"] is None and op["sig"]:
                cnt[op["eng"]] = cnt.get(op["eng"], 0) + 1
                op["val"] = cnt[op["eng"]]

    def emit(self, engname, engobj, sems, final_waits=()):
        waited = {}
        ops = self.ops
        for i in self.order:
            op = ops[i]
            if op["eng"] != engname:
                continue
            need = {}
            for j in op["xdeps"]:
                p = ops[j]
                if p["dma"] is not None:
                    key = ("dma", p["dma"]); val = p["dma_val"]
                else:
                    key = ("eng", p["eng"]); val = p["val"]
                if val > need.get(key, 0):
                    need[key] = val
            for key, val in need.items():
                if waited.get(key, 0) >= val:
                    continue
                engobj.wait_ge(sems[key], val)
                waited[key] = val
            ins = op["fn"](engobj)
            if op["dma"] is not None:
                ins.then_inc(sems[("dma", op["dma"])], 16)
            elif op["sig"]:
                ins.then_inc(sems[("eng", op["eng"])], 1)
        for key in final_waits:
            engobj.wait_ge(sems[key], self.dma_counts[key[1]])


def bc(ap, axis, n):
    l = [list(x) for x in ap.ap]
    l.insert(axis, [0, n])
    return bass.AP(ap.tensor, ap.offset, l)


class Arena:
    def __init__(self, t32, nbytes):
        self.t32 = t32
        self.t16 = t32.bitcast(BF16)
        self.cap = nbytes
        self.off = 0
        self.recs = []

    def alloc(self, dt, n, names=()):
        size = 4 if dt == F32 else 2
        off = (self.off + 63) // 64 * 64
        assert off + n * size <= self.cap, ("arena overflow", off, n * size, self.cap)
        self.off = off + n * size
        self.recs.append((off, off + n * size, list(names)))
        t = self.t32 if dt == F32 else self.t16
        return t[:, off // size: off // size + n]


def build_nc(debug=False):
    nc = bass.Bass("TRN2", target_bir_lowering=False)

    def din(name, shape, dt=F32):
        return nc.dram_tensor(name, shape, dt, kind="ExternalInput").ap()

    xh = din("xh", [T + 128, D])
    w_in = din("w_in", [D, 2304]); w_out = din("w_out", [D, D]); w_up = din("w_up", [D, 4096]); w_down = din("w_down", [4096, D])
    g1 = din("g1", [1, D]); g2 = din("g2", [1, D]); gq = din("gq", [1, 64]); gk = din("gk", [1, 64])
    sinks = din("sinks", [1, 8]); cw = din("cw", [128, 12]); gA = din("gA", [1, 512]); gC = din("gC", [128, 4])
    ident_d = din("ident", [128, 128], BF16); maskg_d = din("maskg", [128, 256], BF16); mask0_d = din("mask0", [128, 256], BF16)
    out = nc.dram_tensor("out", [T, D], F32, kind="ExternalOutput").ap()
    x1s = nc.dram_tensor("x1s", [T, D], F32, kind="ExternalOutput" if debug else "Internal").ap()

    w_in_v = w_in.rearrange("(c p) n -> p c n", p=128)
    w_out_v = w_out.rearrange("(c p) n -> p c n", p=128)
    w_up_v = w_up.rearrange("(c p) n -> p c n", p=128)
    w_down_v = w_down.rearrange("(f p) n -> p f n", p=128)

    P = Prog()
    with ExitStack() as es:
        WA = es.enter_context(nc.sbuf_tensor("WA", [128, 32768], BF16))
        WUPt = es.enter_context(nc.sbuf_tensor("WUP", [128, 32768], BF16))
        ARB = 212863 - 2 * 65536 - 100
        ARB = ARB // 64 * 64
        art = es.enter_context(nc.sbuf_tensor("arena", [128, ARB // 4], F32))
        ps = [es.enter_context(nc.psum_tensor("ps%d" % b, [128, 512], F32)) for b in range(8)]
        ps16 = [p.bitcast(BF16) for p in ps]
        A = Arena(art, ARB)

        Win = WA[:, 0:18432].rearrange("p (c n) -> p c n", c=8)
        Wout = WA[:, 18432:26624].rearrange("p (c n) -> p c n", c=8)
        hT = WA[:, 26624:30720].rearrange("p (c n) -> p c n", c=8)
        qT = [WA[:, 30720 + i * 512: 30720 + (i + 1) * 512].rearrange("p (c n) -> p c n", c=4) for i in range(4)]
        Wd = WA[:, :].rearrange("p (f n) -> p f n", f=32)
        Wup = WUPt[:, :].rearrange("p (c n) -> p c n", c=8)

        ident = A.alloc(BF16, 128)
        ones = A.alloc(BF16, 32)[:, 0:1]
        stat = A.alloc(F32, 256)
        _sc = [0]

        def st(n):
            o = _sc[0]; _sc[0] += n
            assert _sc[0] <= 256
            return stat[:, o:o + n]
        negB = st(1); ones_unused = st(1); esink_h = st(8); esink_s = st(8); sink_t = st(8)
        ss1 = st(4); l1 = st(4); rs1 = st(4)
        ssqk = st(40); lqk = st(40); rqk = st(40)
        den = st(16); rden = st(16)
        ssA = st(4); lA = st(4); rA = st(4)
        lC = st(8); rC = st(8)
        bmax = st(1)
        cwt = st(12); gCt = st(4)
        mark = A.off

        g1t = A.alloc(F32, 1024, ["g1t"])
        xn = [A.alloc(F32, 1024, ["xn%d" % i]) for i in range(2)]
        hb = [A.alloc(BF16, 1024, ["hb%d" % i]) for i in range(NHB)]
        Csb = A.alloc(F32, 512, ["Csb"]); ubuf = A.alloc(F32, 516, ["ubuf"]); acc = A.alloc(F32, 512, ["acc"]); yc = A.alloc(F32, 512, ["yc"])
        sqc = A.alloc(BF16, 512, ["sqc"])
        carry = A.alloc(F32, 8, ["carry%d" % i for i in range(4)]).rearrange("p (c n) -> p c n", c=4)
        sq32 = A.alloc(F32, 640, ["sq32a", "sq32b"]); tmp32 = A.alloc(F32, 640, ["tmp32a", "tmp32b"])
        qkn = [A.alloc(BF16, 768, ["qkn%d" % i]) for i in range(2)]
        gqk = A.alloc(F32, 768, ["gqk"])
        gq_t = A.alloc(F32, 64, ["gq_t"]); gk_t = A.alloc(F32, 64, ["gk_t"]); prod = A.alloc(F32, 64, ["prod"]); prod2 = A.alloc(F32, 64, ["prod2"])
        gAt = A.alloc(F32, 512, ["gAt"])
        maskg = A.alloc(BF16, 256, ["maskg"]); mask0 = A.alloc(BF16, 256, ["mask0"])
        kTr = A.alloc(BF16, 2 * 8 * 128, ["kT%d" % i for i in range(8)]).rearrange("p (k s n) -> p k s n", k=2, s=8)
        vr = A.alloc(BF16, 8 * 2 * 66, ["vr%d" % i for i in range(8)] + ["vr_ones"]).rearrange("p (s k d) -> p s k d", s=8, k=2)
        pT_l = [A.alloc(BF16, 2048, ["pT%d_%d" % (k, i) for i in range(4)]) for k in range(NPT)]
        yattn_l = [A.alloc(F32, 512, ["yattn%d_0" % k, "yattn%d_1" % k]) for k in range(NPT)]
        ya1 = A.alloc(BF16, 512, ["ya"])
        ya = [ya1, ya1]
        yT = A.alloc(BF16, 2048, ["yT%d" % i for i in range(4)]).rearrange("p (c n) -> p c n", c=4)
        ycT = A.alloc(BF16, 2048, ["ycT%d" % i for i in range(4)]).rearrange("p (c n) -> p c n", c=4)
        xr = [A.alloc(F32, 1024, ["xr%d" % i]) for i in range(2)]
        x1t = [A.alloc(F32, 1024, ["x1t%d_0" % i, "x1t%d_1" % i]) for i in range(NX1)]
        recsA = list(A.recs)
        endA = A.off

        import os as _os
        pools = {"conv": [0, 1, 2, 3, 4, 5, 6], "main": [0, 1, 2, 3, 4, 5, 6], "all": [0, 1, 2, 3, 4, 5, 6, 7]}
        if _os.environ.get("KPOOLS"):
            for part in _os.environ["KPOOLS"].split(";"):
                k, v = part.split(":")
                pools[k] = [int(t) for t in v.split(",")]
        if pools["conv"] == pools["main"]:
            bank_shared = True
        else:
            bank_shared = False
        bank_rr = {"conv": 0, "main": 0, "all": 0}

        def nb(pool="main"):
            if pool == "conv" and bank_shared:
                pool = "main"
            lst = pools[pool]
            b = lst[bank_rr[pool] % len(lst)]
            bank_rr[pool] += 1
            return b

        def fsz(ap):
            n = 1
            for d in ap.shape[1:]:
                n *= d
            return n

        def is_ps(ap):
            try:
                return ap.tensor.name.startswith("ps")
            except Exception:
                return False

        def dma(q, out_ap, in_ap, key, reads=(), writes=(), prio=0):
            nbytes = fsz(out_ap) * out_ap.shape[0] * 4
            if q == "sp":
                busy, lat = 120.0, 2500.0 + nbytes / 200.0
                if key.startswith("c_") and key not in ("c_id", "c_g1"):
                    lat = 22000.0
            else:
                busy, lat = 1500.0, 3500.0 + nbytes / 250.0
            P.add(q, lambda e: e.dma_start(out=out_ap, in_=in_ap), reads=reads, writes=writes, dma=key, busy=busy, lat=lat, prio=prio)

        def act(out_ap, in_ap, func, reads, writes, **kw):
            d = 230.0 + 0.83 * fsz(in_ap)
            P.add("act", lambda e: e.activation(out=out_ap, in_=in_ap, func=func, **kw), reads=reads, writes=writes, busy=d, lat=d + 60)

        def vdur(eng, out_ap, ins):
            n = fsz(out_ap)
            if eng == "dve":
                per = 1.04
                if out_ap.dtype == BF16 and all(a.dtype == BF16 for a in ins):
                    per = 0.55
                d = 110.0 + per * n + (70.0 if any(is_ps(a) for a in ins) else 0.0)
            else:
                d = 200.0 + 1.9 * n
            return d

        def copy(eng, out_ap, in_ap, reads, writes):
            if eng == "act":
                act(out_ap, in_ap, AF.Copy, reads, writes)
            else:
                d = vdur(eng, out_ap, [in_ap])
                P.add(eng, lambda e: e.tensor_copy(out=out_ap, in_=in_ap), reads=reads, writes=writes, busy=d, lat=d + 60)

        def tt(eng, out_ap, in0, in1, op, reads, writes):
            d = vdur(eng, out_ap, [in0, in1])
            P.add(eng, lambda e: e.tensor_tensor(out=out_ap, in0=in0, in1=in1, op=op), reads=reads, writes=writes, busy=d, lat=d + 60)

        def ts(eng, out_ap, in0, s1, op0, reads, writes, s2=None, op1=None):
            d = vdur(eng, out_ap, [in0])
            if op1 is None:
                P.add(eng, lambda e: e.tensor_scalar(out=out_ap, in0=in0, scalar1=s1, scalar2=None, op0=op0), reads=reads, writes=writes, busy=d, lat=d + 60)
            else:
                P.add(eng, lambda e: e.tensor_scalar(out=out_ap, in0=in0, scalar1=s1, scalar2=s2, op0=op0, op1=op1), reads=reads, writes=writes, busy=d, lat=d + 60)

        def stt(out_ap, in0, scalar, in1, op0, op1, reads, writes):
            d = vdur("dve", out_ap, [in0, in1])
            P.add("dve", lambda e: e.scalar_tensor_tensor(out=out_ap, in0=in0, scalar=scalar, in1=in1, op0=op0, op1=op1), reads=reads, writes=writes, busy=d, lat=d + 60)

        def mm(out_ap, lhsT, rhs, start, stop, reads, writes, **kw):
            n = fsz(rhs)
            if lhsT.dtype == F32:
                d = 135.0
            elif lhsT.shape[0] == 64:
                d = 200.0
            else:
                d = max(62.0, 16.0 + 0.405 * n)
            P.add("pe", lambda e: e.matmul(out_ap, lhsT=lhsT, rhs=rhs, start=start, stop=stop, **kw), reads=reads, writes=writes, busy=d, lat=d + 250)

        def tr(out_ap, in_ap, reads, writes):
            P.add("pe", lambda e: e.transpose(out=out_ap, in_=in_ap, identity=ident), reads=list(reads) + ["ident"], writes=writes, busy=90.0, lat=340.0)

        def rstd(ss_ap, l_ap, r_ap, n, names):
            act(l_ap, ss_ap, AF.Ln, [names[0]], [names[1]], scale=1.0 / n, bias=EPS)
            act(r_ap, l_ap, AF.Exp, [names[1]], [names[2]], scale=-0.5)

        def wload(dst, src, key, name, reads=()):
            dma("pool", dst, src, key, reads=reads, writes=[name])
        wload(Win[:, :, 0:768], w_in_v[:, :, 0:768], "wi0", "Win0")
        dma("sp", ident, ident_d, "c_id", writes=["ident"], prio=-1)
        dma("sp", g1t, g1.partition_broadcast(128), "c_g1", writes=["g1t"], prio=-1)
        dma("sp", gq_t, gq.partition_broadcast(128), "c_gq", writes=["gq_t"])
        dma("sp", gk_t, gk.partition_broadcast(128), "c_gk", writes=["gk_t"])
        dma("sp", sink_t, sinks.partition_broadcast(128), "c_sk", writes=["sink_t"])
        dma("sp", cwt, cw, "c_cw", writes=["cwt"])
        dma("sp", gCt, gC, "c_gC", writes=["gCt"])
        dma("sp", gAt, gA.partition_broadcast(128), "c_gA", writes=["gAt"])
        dma("sp", maskg, maskg_d, "c_mg", writes=["maskg"])
        dma("sp", mask0, mask0_d, "c_m0", writes=["mask0"])
        wload(Win[:, :, O3:O4], w_in_v[:, :, O3:O4], "wi2", "Win2")
        wload(Win[:, :, O4:2304], w_in_v[:, :, O4:2304], "wi3", "Win3")
        wload(Win[:, :, O2:O3], w_in_v[:, :, O2:O3], "wi1", "Win1")
        wload(Wout[:, :, :], w_out_v[:, :, :], "wo", "Wout")

        P.add("pool", lambda e: e.memset(ones, 1.0), writes=["ones"])
        P.add("pool", lambda e: e.memset(vr[:, :, :, 64:65], 1.0), writes=["vr_ones"])
        cw3 = cwt.rearrange("p (c j) -> p c j", c=4)
        gqk3 = gqk.rearrange("p (h d) -> p h d", d=64)
        copy("dve", gqk3[:, 0:8, :], bc(gq_t, 1, 8), ["gq_t"], ["gqk"])
        ts("dve", gqk3[:, 8:12, :], bc(gk_t, 1, 4), 0.125, ALU.mult, ["gk_t"], ["gqk"])
        tt("dve", prod, gq_t, gk_t, ALU.mult, ["gq_t", "gk_t"], ["prod"])
        ts("dve", prod2, prod, -1.0, ALU.mult, ["prod"], ["prod2"])
        tt("dve", prod, prod, prod2, ALU.max, ["prod", "prod2"], ["prod"])
        P.add("dve", lambda e: e.tensor_reduce(out=bmax, in_=prod, axis=AX.X, op=ALU.max), reads=["prod"], writes=["bmax"])
        ts("dve", negB, bmax, -8.0, ALU.mult, ["bmax"], ["negB"])
        act(esink_h, sink_t, AF.Exp, ["sink_t", "negB"], ["esink_h"], bias=negB, scale=1.0)
        copy("dve", esink_s.rearrange("p (k f c) -> p k f c", k=2, f=2), esink_h.rearrange("p (k c f) -> p k f c", k=2, c=2), ["esink_h"], ["esink_s"])

        def tiles_of(m):
            if m < 0:
                return [(0, 0)]
            return [(s, 1 + 4 * m + s) for s in range(4)]

        def NTs(m):
            P.tag = 'NT%d' % m
            for (s, ti) in tiles_of(m):
                sl = ti % 2; q4 = ti % 4; hs = ti % NHB
                col = s * 128
                dma("sp", xn[sl], xh[ti * 128:(ti + 1) * 128, :], "xn%d" % sl, writes=["xn%d" % sl], prio=-1)
                act(hb[hs], xn[sl], AF.Square, ["xn%d" % sl], ["hb%d" % hs, "ss1_%d" % q4], accum_out=ss1[:, q4:q4 + 1])
                rstd(ss1[:, q4:q4 + 1], l1[:, q4:q4 + 1], rs1[:, q4:q4 + 1], D, ("ss1_%d" % q4, "l1_%d" % q4, "rs1_%d" % q4))
                stt(hb[hs], xn[sl], rs1[:, q4:q4 + 1], g1t, ALU.mult, ALU.mult, ["xn%d" % sl, "rs1_%d" % q4, "g1t"], ["hb%d" % hs])
                b = nb()
                for c in range(8):
                    tr(ps16[b][:, c * 128:(c + 1) * 128], hb[hs][:, c * 128:(c + 1) * 128], ["hb%d" % hs], ["ps%d" % b])
                copy("act", hT[:, :, col:col + 128], ps16[b][:, :].rearrange("p (c n) -> p c n", c=8), ["ps%d" % b], ["hT%d" % s])

        def Zs(m):
            P.tag = 'Z%d' % m
            for (s, ti) in tiles_of(m):
                sl = ti % 2; q4 = ti % 4; s8 = ti % 8
                col = s * 128
                bq = nb()
                for kc in range(8):
                    mm(ps[bq][:, :], hT[:, kc, col:col + 128], Win[:, kc, 0:512], kc == 0, kc == 7, ["hT%d" % s, "Win0"], ["ps%d" % bq])
                bk = nb()
                for kc in range(8):
                    mm(ps[bk][:, 0:256], hT[:, kc, col:col + 128], Win[:, kc, 512:768], kc == 0, kc == 7, ["hT%d" % s, "Win0"], ["ps%d" % bk])
                act(sq32[:, 0:512], ps[bq][:, :], AF.Square, ["ps%d" % bq], ["sq32a"])
                act(sq32[:, 512:640], ps[bk][:, 0:128], AF.Square, ["ps%d" % bk], ["sq32b"])
                sv = ssqk[:, q4 * 10:(q4 + 1) * 10]; lv = lqk[:, q4 * 10:(q4 + 1) * 10]; rv = rqk[:, q4 * 10:(q4 + 1) * 10]
                P.add("dve", lambda e, sv=sv: e.tensor_reduce(out=sv, in_=sq32.rearrange("p (h d) -> p h d", d=64), axis=AX.X, op=ALU.add),
                      reads=["sq32a", "sq32b"], writes=["ssqk%d" % q4], busy=780.0, lat=840.0)
                rstd(sv, lv, rv, 64, ("ssqk%d" % q4, "lqk%d" % q4, "rqk%d" % q4))
                tt("dve", tmp32[:, 0:512].rearrange("p (h d) -> p h d", d=64), ps[bq][:, :].rearrange("p (h d) -> p h d", d=64),
                   bc(rv[:, 0:8], 2, 64), ALU.mult, ["ps%d" % bq, "rqk%d" % q4], ["tmp32a"])
                tt("dve", tmp32[:, 512:640].rearrange("p (h d) -> p h d", d=64), ps[bk][:, 0:128].rearrange("p (h d) -> p h d", d=64),
                   bc(rv[:, 8:10], 2, 64), ALU.mult, ["ps%d" % bk, "rqk%d" % q4], ["tmp32b"])
                qn = qkn[sl]
                tt("pool", qn[:, 0:512], tmp32[:, 0:512], gqk[:, 0:512], ALU.mult, ["tmp32a", "gqk"], ["qkn%d" % sl])
                tt("pool", qn[:, 512:768].rearrange("p (k u d) -> p k u d", k=2, u=2),
                   bc(tmp32[:, 512:640].rearrange("p (k d) -> p k d", k=2), 2, 2),
                   gqk[:, 512:768].rearrange("p (k u d) -> p k u d", k=2, u=2), ALU.mult, ["tmp32b", "gqk"], ["qkn%d" % sl])
                copy("act", vr[:, s8, :, 0:64], ps[bk][:, 128:256].rearrange("p (k d) -> p k d", k=2), ["ps%d" % bk], ["vr%d" % s8])
                bt = nb()
                for c in range(6):
                    tr(ps16[bt][:, c * 128:(c + 1) * 128], qn[:, c * 128:(c + 1) * 128], ["qkn%d" % sl], ["ps%d" % bt])
                copy("dve", qT[s], ps16[bt][:, 0:512].rearrange("p (c n) -> p c n", c=4), ["ps%d" % bt], ["qT%d" % s])
                copy("dve", kTr[:, :, s8, :], ps16[bt][:, 512:768].rearrange("p (k n) -> p k n", k=2), ["ps%d" % bt], ["kT%d" % s8])

        def ATTs(m):
            P.tag = 'ATT%d' % m
            for (s, ti) in tiles_of(m):
                sl = ti % 2; q4 = ti % 4
                col = s * 128
                kts = [(ti - 1) % 8, ti % 8]
                pi = ti % NPT
                pT = pT_l[pi]; yattn = yattn_l[pi]
                pT5 = pT.rearrange("p (j b c q) -> p j b c q", j=2, b=4, c=2)
                pTn = ["pT%d_%d" % (pi, b_) for b_ in range(4)]
                yan = ["yattn%d_%d" % (pi, k_) for k_ in range(2)]
                for kv in range(2):
                    bSs = [nb(), nb()]
                    for j in range(2):
                        for half in range(2):
                            bS = bSs[half]
                            lo = half * 64
                            mm(ps[bS][:, j * 256:(j + 1) * 256].rearrange("p (c n) -> p c n", c=2),
                               kTr[lo:lo + 64, kv, kts[j], :], qT[s][lo:lo + 64, 2 * kv:2 * kv + 2, :], True, True,
                               ["kT%d" % kts[j], "qT%d" % s], ["ps%d" % bS])
                    for half in range(2):
                        b4 = kv * 2 + half
                        bS = bSs[half]
                        act(pT5[:, :, b4, :, :], ps[bS][:, :].rearrange("p (j c q) -> p j c q", j=2, c=2), AF.Exp,
                            ["ps%d" % bS, "negB"], [pTn[b4]], bias=negB, scale=1.0)
                mk = mask0 if ti == 1 else maskg
                mkn = "mask0" if ti == 1 else "maskg"
                pT4 = pT.rearrange("p (j r q) -> p j r q", j=2, r=8)
                for kv in range(2):
                    pv = pT4[:, :, 4 * kv:4 * kv + 4, :]
                    tt(MASK_ENG, pv, pv, bc(mk.rearrange("p (j q) -> p j q", j=2), 2, 4), ALU.mult,
                       [pTn[2 * kv], pTn[2 * kv + 1], mkn], [pTn[2 * kv], pTn[2 * kv + 1]])
                d8 = den[:, sl * 8:(sl + 1) * 8]; r8 = rden[:, sl * 8:(sl + 1) * 8]
                bOs = []
                for kv in range(2):
                    bO = nb(); bOs.append(bO)
                    for half in range(2):
                        for c in range(2):
                            s4 = half * 2 + c; b4 = kv * 2 + half
                            for j in range(2):
                                mm(ps[bO][:, s4 * 128:s4 * 128 + 65], pT5[:, j, b4, c, :], vr[:, kts[j], kv, 0:65], j == 0, j == 1,
                                   [pTn[b4], "vr%d" % kts[j], "vr_ones"], ["ps%d" % bO])
                    o3v = ps[bO][:, :].rearrange("p (s x) -> p s x", s=4)
                    tt("dve", d8[:, kv * 4:kv * 4 + 4], o3v[:, :, 64], esink_s[:, kv * 4:kv * 4 + 4], ALU.add, ["ps%d" % bO, "esink_s"], ["den%d_%d" % (sl, kv)])
                P.add("dve", lambda e, d8=d8, r8=r8: e.reciprocal(out=r8, in_=d8), reads=["den%d_0" % sl, "den%d_1" % sl], writes=["rden%d" % sl])
                for kv in range(2):
                    bO = bOs[kv]
                    tt("dve", yattn[:, kv * 256:(kv + 1) * 256].rearrange("p (c f d) -> p f c d", c=2, f=2),
                       ps[bO][:, :].rearrange("p (f c x) -> p f c x", f=2, c=2)[:, :, :, 0:64],
                       bc(r8[:, kv * 4:kv * 4 + 4].rearrange("p (f c) -> p f c", f=2), 3, 64), ALU.mult,
                       ["ps%d" % bO, "rden%d" % sl], [yan[kv]])
                act(tmp32[:, 0:512], yattn, AF.Square, yan, ["tmp32a", "ssA%d" % q4], accum_out=ssA[:, q4:q4 + 1])
                rstd(ssA[:, q4:q4 + 1], lA[:, q4:q4 + 1], rA[:, q4:q4 + 1], 512, ("ssA%d" % q4, "lA%d" % q4, "rA%d" % q4))
                stt(ya[sl], yattn, rA[:, q4:q4 + 1], gAt, ALU.mult, ALU.mult, yan + ["rA%d" % q4, "gAt"], ["ya"])
                bt = nb()
                for c in range(4):
                    tr(ps16[bt][:, c * 128:(c + 1) * 128], ya[sl][:, c * 128:(c + 1) * 128], ["ya"], ["ps%d" % bt])
                copy("act", yT[:, :, col:col + 128], ps16[bt][:, 0:512].rearrange("p (c n) -> p c n", c=4), ["ps%d" % bt], ["yT%d" % s])

        def CONVs(m):
            P.tag = 'CONV%d' % m
            halo = m < 0
            N = 128 if halo else 512
            hts = ["hT0"] if halo else ["hT0", "hT1", "hT2", "hT3"]
            m2 = m % 2
            for i in range(4):
                def grp(coff, wname):
                    b = nb("conv")
                    for kc in range(8):
                        mm(ps[b][:, 0:N], Win[:, kc, coff + 128 * i: coff + 128 * (i + 1)], hT[:, kc, 0:N], kc == 0, kc == 7, hts + [wname], ["ps%d" % b])
                    return b
                bC = grp(O3, "Win2")
                bX = grp(O4, "Win3")
                copy("act", Csb[:, 0:N], ps[bC][:, 0:N], ["ps%d" % bC], ["Csb"])
                if not halo:
                    copy("pool", ubuf[:, 0:2], carry[:, i, :], ["carry%d" % i], ["ubuf"])
                tt("dve", ubuf[:, 2:2 + N], Csb[:, 0:N], ps[bX][:, 0:N], ALU.mult, ["Csb", "ps%d" % bX], ["ubuf"])
                copy("pool", carry[:, i, :], ubuf[:, N:N + 2], ["ubuf"], ["carry%d" % i])
                if halo:
                    continue
                bB = grp(O2, "Win1")
                ts("dve", acc, ubuf[:, 2:514], cw3[:, i, 2:3], ALU.mult, ["ubuf", "cwt"], ["acc"])
                stt(acc, ubuf[:, 1:513], cw3[:, i, 1:2], acc, ALU.mult, ALU.add, ["ubuf", "cwt", "acc"], ["acc"])
                stt(acc, ubuf[:, 0:512], cw3[:, i, 0:1], acc, ALU.mult, ALU.add, ["ubuf", "cwt", "acc"], ["acc"])
                tt("dve", yc, acc, ps[bB][:, :], ALU.mult, ["acc", "ps%d" % bB], ["yc"])
                act(sqc, yc, AF.Square, ["yc"], ["sqc"])
                for s in range(4):
                    mm(ps[7][:, s:s + 1], sqc[:, s * 128:(s + 1) * 128], ones, (i == 0 and s == 0), (i == 3 and s == 3), ["sqc", "ones"], ["ps7"], skip_group_check=True)
                act(ycT[:, i, :], yc, AF.Identity, ["yc", "gCt"], ["ycT%d" % i], scale=gCt[:, i:i + 1])
            if not halo:
                lv = lC[:, m2 * 4:m2 * 4 + 4]; rv = rC[:, m2 * 4:m2 * 4 + 4]
                act(lv, ps[7][:, 0:4], AF.Ln, ["ps7"], ["lC%d" % m2], scale=1.0 / 512, bias=EPS)
                act(rv, lv, AF.Exp, ["lC%d" % m2], ["rC%d" % m2], scale=-0.5)

        def OUTs(m):
            P.tag = 'OUT%d' % m
            m2 = m % 2
            for (s, ti) in tiles_of(m):
                sl = ti % 2
                col = s * 128
                dma("sp", xr[sl], xh[ti * 128:(ti + 1) * 128, :], "xr%d" % sl, writes=["xr%d" % sl])
                for n in range(2):
                    bA = nb()
                    for c in range(4):
                        mm(ps[bA][:, :], yT[:, c, col:col + 128], Wout[:, c, n * 512:(n + 1) * 512], c == 0, c == 3, ["yT%d" % s, "Wout"], ["ps%d" % bA])
                    bCc = nb()
                    for c in range(4):
                        mm(ps[bCc][:, :], ycT[:, c, col:col + 128], Wout[:, 4 + c, n * 512:(n + 1) * 512], c == 0, c == 3, ["ycT%d" % c, "Wout"], ["ps%d" % bCc])
                    x1 = ti % NX1
                    xs = x1t[x1][:, n * 512:(n + 1) * 512]
                    stt(xs, ps[bCc][:, :], rC[:, m2 * 4 + s:m2 * 4 + s + 1], xr[sl][:, n * 512:(n + 1) * 512], ALU.mult, ALU.add,
                        ["ps%d" % bCc, "rC%d" % m2, "xr%d" % sl], ["x1t%d_%d" % (x1, n)])
                    tt("dve", xs, xs, ps[bA][:, :], ALU.add, ["x1t%d_%d" % (x1, n), "ps%d" % bA], ["x1t%d_%d" % (x1, n)])
                x1 = ti % NX1
                dma("sp", x1s[(ti - 1) * 128:ti * 128, :], x1t[x1], "x1st%d" % x1, reads=["x1t%d_0" % x1, "x1t%d_1" % x1], writes=["x1s%d" % ti])

        NTs(-1); Zs(-1); CONVs(-1)
        NTs(0)
        for m in range(NM):
            Zs(m)
            CONVs(m)
            ATTs(m)
            if m + 1 < NM:
                NTs(m + 1)
            OUTs(m)
            if m < 4:
                P.tag = "WUP"
                for j in (2 * m, 2 * m + 1):
                    wload(Wup[:, :, j * 512:(j + 1) * 512], w_up_v[:, :, j * 512:(j + 1) * 512], "wu%d" % j, "Wup%d" % j, reads=["x1s%d" % (1 + 4 * m)])

        A.off = mark
        nA = len(A.recs)
        g2t = A.alloc(F32, 1024, ["g2t"])
        xn2 = [A.alloc(F32, 1024, ["xn2_%d" % i]) for i in range(2)]
        hm = [A.alloc(BF16, 1024, ["hm%d" % i]) for i in range(2)]
        hmT = [None, None]
        hmT[0] = A.alloc(BF16, 4096, ["hmT0_%d" % i for i in range(4)]).rearrange("p (c n) -> p c n", c=8)
        r32 = [A.alloc(F32, 512, ["r32_%d" % i]) for i in range(2)]
        hmT[1] = A.alloc(BF16, 4096, ["hmT1_%d" % i for i in range(4)]).rearrange("p (c n) -> p c n", c=8)
        xr2 = [A.alloc(F32, 1024, ["xr2_%d_0" % i, "xr2_%d_1" % i]) for i in range(2)]
        actT = A.alloc(BF16, 32 * 512, ["actT%d" % i for i in range(32)]).rearrange("p (f n) -> p f n", f=32)
        ss2 = st(4); l2 = st(4); rs2 = st(4)
        for (b0, b1, bn) in A.recs[nA:]:
            for (a0, a1, an) in recsA:
                if a0 < b1 and b0 < a1:
                    for nm in bn:
                        P.alias.setdefault(nm, []).extend(an)
        wa_in = ["Win0", "Win1", "Win2", "Win3"]
        wa_out = ["Wout"]
        wa_sp = ["hT%d" % i for i in range(4)] + ["qT%d" % i for i in range(4)]
        P.alias["Wd0"] = list(wa_in); P.alias["Wd1"] = list(wa_in); P.alias["Wd2"] = wa_in + wa_out; P.alias["Wd3"] = wa_out + wa_sp

        for j in range(4):
            dma("pool", Wd[:, 8 * j:8 * j + 8, :], w_down_v[:, 8 * j:8 * j + 8, :], "wd%d" % j, writes=["Wd%d" % j])
        dma("sp", g2t, g2.partition_broadcast(128), "c_g2", writes=["g2t"])

        def NT2(m):
            P.tag = 'NT2%d' % m
            mb = m % 2
            for (s, ti) in tiles_of(m):
                sl = ti % 2; q4 = ti % 4
                col = s * 128
                dma("sp", xn2[sl], x1s[(ti - 1) * 128:ti * 128, :], "xn2_%d" % sl, reads=["x1s%d" % ti], writes=["xn2_%d" % sl])
                act(hm[sl], xn2[sl], AF.Square, ["xn2_%d" % sl], ["hm%d" % sl, "ss2_%d" % q4], accum_out=ss2[:, q4:q4 + 1])
                rstd(ss2[:, q4:q4 + 1], l2[:, q4:q4 + 1], rs2[:, q4:q4 + 1], D, ("ss2_%d" % q4, "l2_%d" % q4, "rs2_%d" % q4))
                stt(hm[sl], xn2[sl], rs2[:, q4:q4 + 1], g2t, ALU.mult, ALU.mult, ["xn2_%d" % sl, "rs2_%d" % q4, "g2t"], ["hm%d" % sl])
                b = nb("all")
                for c in range(8):
                    tr(ps16[b][:, c * 128:(c + 1) * 128], hm[sl][:, c * 128:(c + 1) * 128], ["hm%d" % sl], ["ps%d" % b])
                copy("act", hmT[mb][:, :, col:col + 128], ps16[b][:, :].rearrange("p (c n) -> p c n", c=8), ["ps%d" % b], ["hmT%d_%d" % (mb, s)])

        def UP(m):
            P.tag = 'UP%d' % m
            mb = m % 2
            hn = ["hmT%d_%d" % (mb, s) for s in range(4)]
            for f in range(32):
                b = nb("all")
                for kc in range(8):
                    mm(ps[b][:, :], Wup[:, kc, f * 128:(f + 1) * 128], hmT[mb][:, kc, :], kc == 0, kc == 7, hn + ["Wup%d" % (f // 4)], ["ps%d" % b])
                r = r32[f % 2]
                act(r, ps[b][:, :], AF.Relu, ["ps%d" % b], ["r32_%d" % (f % 2)])
                tt("dve" if f % 2 == 0 else "pool", actT[:, f, :], r, r, ALU.mult, ["r32_%d" % (f % 2)], ["actT%d" % f])

        def DOWN(m):
            P.tag = 'DOWN%d' % m
            for (s, ti) in tiles_of(m):
                sl = ti % 2
                col = s * 128
                dma("sp", xr2[sl], x1s[(ti - 1) * 128:ti * 128, :], "xr2_%d" % sl, reads=["x1s%d" % ti], writes=["xr2_%d_0" % sl, "xr2_%d_1" % sl])
                for n in range(2):
                    b = nb("all")
                    for f in range(32):
                        mm(ps[b][:, :], actT[:, f, col:col + 128], Wd[:, f, n * 512:(n + 1) * 512], f == 0, f == 31, ["actT%d" % f, "Wd%d" % (f // 8)], ["ps%d" % b])
                    xs = xr2[sl][:, n * 512:(n + 1) * 512]
                    tt("dve", xs, xs, ps[b][:, :], ALU.add, ["xr2_%d_%d" % (sl, n), "ps%d" % b], ["xr2_%d_%d" % (sl, n)])
                dma("sp", out[(ti - 1) * 128:ti * 128, :], xr2[sl], "ost%d" % sl, reads=["xr2_%d_0" % sl, "xr2_%d_1" % sl])

        NT2(0)
        for m in range(NM):
            UP(m)
            if m + 1 < NM:
                NT2(m + 1)
            DOWN(m)

        P.finalize()
        sems = {}
        for k in P.dma_counts:
            sems[("dma", k)] = es.enter_context(nc.semaphore("d_" + k))
        for k in ENGS:
            sems[("eng", k)] = es.enter_context(nc.semaphore("e_" + k))
        with nc.Block() as block:
            @block.sync
            def _(e):
                P.emit("sp", e, sems, final_waits=[("dma", "ost0"), ("dma", "ost1")])

            @block.gpsimd
            def _(e):
                P.emit("pool", e, sems)

            @block.scalar
            def _(e):
                P.emit("act", e, sems)

            @block.vector
            def _(e):
                P.emit("dve", e, sems)

            @block.tensor
            def _(e):
                P.emit("pe", e, sems)
    return nc


_NC_CACHE = {}


def prep_inputs(x, attn_norm_g, w_in, q_norm_g, k_norm_g, sinks, conv_w, attn_out_g, conv_out_g, w_out, mlp_norm_g, w_up, w_down):
    f = lambda a: np.ascontiguousarray(np.asarray(a, dtype=np.float32))
    x = f(x)
    B, S, _ = x.shape
    assert (B, S) == (4, 8192)
    bf = ml_dtypes.bfloat16
    kk = np.arange(128)[:, None]; qq = np.arange(128)[None, :]
    m_prev = (kk > qq).astype(np.float32); m_cur = (kk <= qq).astype(np.float32)
    maskg = np.concatenate([m_prev, m_cur], axis=1).astype(bf)
    mask0_first = np.concatenate([np.zeros_like(m_prev), m_cur], axis=1).astype(bf)
    ident = np.eye(128, dtype=np.float32).astype(bf)
    common = dict(
        w_in=f(w_in[0]), w_out=f(w_out[0]), w_up=f(w_up[0]), w_down=f(w_down[0]),
        g1=f(attn_norm_g[0]).reshape(1, D), g2=f(mlp_norm_g[0]).reshape(1, D),
        gq=f(q_norm_g[0]).reshape(1, 64), gk=f(k_norm_g[0]).reshape(1, 64), sinks=f(sinks[0]).reshape(1, 8),
        cw=np.ascontiguousarray(f(conv_w[0]).reshape(3, 4, 128).transpose(2, 1, 0).reshape(128, 12)),
        gA=f(attn_out_g[0]).reshape(1, 512),
        gC=np.ascontiguousarray(f(conv_out_g[0]).reshape(4, 128).T),
        ident=ident, maskg=maskg,
    )
    in_maps = []
    for c in range(8):
        b, half = c // 2, c % 2
        if half == 0:
            xh = np.concatenate([np.zeros((128, D), np.float32), x[b, 0:T]], axis=0)
            m0 = mask0_first
        else:
            xh = x[b, T - 128:2 * T]
            m0 = maskg
        d = dict(common)
        d["xh"] = np.ascontiguousarray(xh)
        d["mask0"] = m0
        in_maps.append(d)
    return in_maps


def kernel(**inputs):
    in_maps = prep_inputs(**inputs)
    B, S = 4, 8192
    if "nc" not in _NC_CACHE:
        _NC_CACHE["nc"] = build_nc()
    nc = _NC_CACHE["nc"]
    res = run_bass_kernel_spmd(nc, in_maps, core_ids=list(range(8)))
    outp = np.empty((B, S, D), np.float32)
    for c in range(8):
        b, half = c // 2, c % 2
        outp[b, half * T:(half + 1) * T] = res.results[c]["out"]
    return outp
```

```python
import numpy as np
import ml_dtypes
from contextlib import ExitStack
import concourse.bass as bass
import concourse.mybir as mybir
from concourse.bass_utils import run_bass_kernel_spmd

F32 = mybir.dt.float32
BF16 = mybir.dt.bfloat16
AF = mybir.ActivationFunctionType
ALU = mybir.AluOpType
AX = mybir.AxisListType

D = 1024
T = 4096
NTILE = T // 128
NM = T // 512
O1, O2, O3, O4 = 640, 768, 1280, 1792
EPS = 1e-6
import os as _os0
ENGS = ["pe", "act", "dve", "pool", "sp"]
MASK_ENG = _os0.environ.get("KMASK", "dve")
NPT = int(_os0.environ.get("KNPT", "1"))
NX1 = int(_os0.environ.get("KNX1", "2"))
NHB = int(_os0.environ.get("KNHB", "2"))


class Prog:
    XLAT = float(_os0.environ.get('KXLAT', '700'))
    WIN = float(_os0.environ.get('KWIN', '100'))
    USE_BL = int(_os0.environ.get('KBL', '1'))
    SE_ALL = int(_os0.environ.get('KSEALL', '1'))
    DMA_BW = float(_os0.environ.get('KDMABW', '200'))

    def __init__(self, same_engine_raw=True, do_schedule=True):
        self.ops = []
        self.buf = {}
        self.same_engine_raw = same_engine_raw
        self.do_schedule = do_schedule
        self.last_on_eng = {}
        self.last_dma = {}
        self.pending_bar = {}
        self.bar_first = {}
        self.alias = {}

    def add(self, eng, fn, reads=(), writes=(), dma=None, busy=200.0, lat=None, prio=0, dbytes=0):
        i = len(self.ops)
        op = dict(eng=eng, fn=fn, deps={}, dma=dma, sig=False, idx=i, busy=float(busy), lat=float(busy if lat is None else lat), tag=getattr(self, 'tag', ''), prio=prio, dbytes=dbytes)
        deps = op["deps"]

        def dep(j, kind):
            if j is None or j == i:
                return
            if deps.get(j) != "raw":
                deps[j] = kind
        for r in reads:
            st = self.buf.setdefault(r, dict(w=None, r=[]))
            dep(st["w"], "raw")
        for w in writes:
            st = self.buf.setdefault(w, dict(w=None, r=[]))
            dep(st["w"], "waw")
            for rr in st["r"]:
                dep(rr, "war")
        for r in reads:
            self.buf[r]["r"].append(i)
        for w in writes:
            st = self.buf[w]
            st["w"] = i
            st["r"] = []
        for nm in list(reads) + list(writes):
            if nm in self.alias:
                for a in self.alias.pop(nm):
                    st = self.buf.get(a)
                    if st is not None:
                        dep(st["w"], "waw")
                        for rr in st["r"]:
                            dep(rr, "war")
        if eng in self.pending_bar:
            for j in self.pending_bar.pop(eng):
                dep(j, "bar")
            self.bar_first[eng] = i
        elif eng in self.bar_first:
            dep(self.bar_first[eng], "bar")
        if dma is not None:
            self.last_dma[dma] = i
        else:
            self.last_on_eng[eng] = i
        self.ops.append(op)
        return i

    def barrier(self):
        deps = list(self.last_on_eng.values()) + list(self.last_dma.values())
        for e in ENGS:
            self.pending_bar[e] = list(deps)

    def schedule(self):
        ops = self.ops
        n = len(ops)
        if not self.do_schedule:
            self.order = list(range(n))
            return
        succ = [[] for _ in range(n)]
        nleft = [0] * n
        for op in ops:
            nleft[op["idx"]] = len(op["deps"])
            for j in op["deps"]:
                succ[j].append(op["idx"])
        blevel = [0.0] * n
        for i in range(n - 1, -1, -1):
            b = blevel[i] + ops[i]["lat"]
            ops[i]["blevel"] = b
            for j in ops[i]["deps"]:
                if b > blevel[j]:
                    blevel[j] = b
        W = self.WIN
        dma_free = {e_: 0.0 for e_ in ENGS}
        tfree = {e: 0.0 for e in ENGS}
        avail = [0.0] * n
        dready = [0.0] * n
        ready = {e: [] for e in ENGS}
        for op in ops:
            if nleft[op["idx"]] == 0:
                ready[op["eng"]].append(op["idx"])
        order = []
        done = 0
        while done < n:
            best = None
            cands = []
            for e in ENGS:
                tf = tfree[e]
                for i in ready[e]:
                    t = dready[i] if dready[i] > tf else tf
                    cands.append((t, i))
                    if best is None or t < best:
                        best = t
            bk = None
            for (t, i) in cands:
                if t <= best + W:
                    key = (ops[i]["prio"], -ops[i]["blevel"] if self.USE_BL else 0, t, i)
                    if bk is None or key < bk:
                        bk = key
                        bt = t
            i = bk[3]
            t = bt
            op = ops[i]
            e = op["eng"]
            ready[e].remove(i)
            tfree[e] = t + op["busy"]
            if op["dma"] is not None and op["dbytes"] > 0:
                ds = t + op["busy"] + 1500.0
                if dma_free[e] > ds:
                    ds = dma_free[e]
                fin_ = ds + op["dbytes"] / (320.0 if e == "pool" else self.DMA_BW)
                dma_free[e] = fin_
                avail[i] = fin_ + 1500.0
            else:
                avail[i] = t + op["lat"]
            op["t_start"] = t
            order.append(i)
            done += 1
            for k in succ[i]:
                nleft[k] -= 1
                if ops[k]["eng"] == e and op["dma"] is None and ops[k]["dma"] is None:
                    if e != "pe" and (ops[k]["deps"][i] == "raw" or self.SE_ALL) and self.same_engine_raw:
                        a = avail[i]
                    else:
                        a = t + op["busy"]
                else:
                    a = avail[i] + self.XLAT
                if a > dready[k]:
                    dready[k] = a
                    ops[k]["crit"] = i
                if nleft[k] == 0:
                    ready[ops[k]["eng"]].append(k)
        self.order = order
        self.est_total = max(avail)

    def finalize(self):
        self.schedule()
        ops = self.ops
        pos = {i: p for p, i in enumerate(self.order)}
        self.dma_counts = {}
        for i in self.order:
            op = ops[i]
            if op["dma"] is not None:
                self.dma_counts[op["dma"]] = self.dma_counts.get(op["dma"], 0) + 16
                op["dma_val"] = self.dma_counts[op["dma"]]
        for op in ops:
            nd = []
            for j, kind in op["deps"].items():
                p = ops[j]
                assert pos[j] < pos[op["idx"]]
                if p["dma"] is None and op["dma"] is None and p["eng"] == op["eng"]:
                    if op["eng"] == "pe":
                        continue
                    if not ((kind == "raw" or self.SE_ALL) and self.same_engine_raw):
                        continue
                nd.append(j)
            latest = {}
            keep = []
            for j in nd:
                p = ops[j]
                if p["dma"] is not None:
                    keep.append(j)
                else:
                    if p["eng"] not in latest or pos[j] > pos[latest[p["eng"]]]:
                        latest[p["eng"]] = j
            keep += list(latest.values())
            op["xdeps"] = keep
            for j in keep:
                ops[j]["sig"] = True
        cnt = {}
        for i in self.order:
            op = ops[i]
            if op["dma"] is None and op["sig"]:
                cnt[op["eng"]] = cnt.get(op["eng"], 0) + 1
                op["val"] = cnt[op["eng"]]

    def emit(self, engname, engobj, sems, final_waits=()):
        waited = {}
        ops = self.ops
        for i in self.order:
            op = ops[i]
            if op["eng"] != engname:
                continue
            need = {}
            for j in op["xdeps"]:
                p = ops[j]
                if p["dma"] is not None:
                    key = ("dma", p["dma"]); val = p["dma_val"]
                else:
                    key = ("eng", p["eng"]); val = p["val"]
                if val > need.get(key, 0):
                    need[key] = val
            for key, val in need.items():
                if waited.get(key, 0) >= val:
                    continue
                engobj.wait_ge(sems[key], val)
                waited[key] = val
            ins = op["fn"](engobj)
            if op["dma"] is not None:
                ins.then_inc(sems[("dma", op["dma"])], 16)
            elif op["sig"]:
                ins.then_inc(sems[("eng", op["eng"])], 1)
        for key in final_waits:
            engobj.wait_ge(sems[key], self.dma_counts[key[1]])


def bc(ap, axis, n):
    l = [list(x) for x in ap.ap]
    l.insert(axis, [0, n])
    return bass.AP(ap.tensor, ap.offset, l)


class Arena:
    def __init__(self, t32, nbytes):
        self.t32 = t32
        self.t16 = t32.bitcast(BF16)
        self.cap = nbytes
        self.off = 0
        self.recs = []

    def alloc(self, dt, n, names=()):
        size = 4 if dt == F32 else 2
        off = (self.off + 63) // 64 * 64
        assert off + n * size <= self.cap, ("arena overflow", off, n * size, self.cap)
        self.off = off + n * size
        self.recs.append((off, off + n * size, list(names)))
        t = self.t32 if dt == F32 else self.t16
        return t[:, off // size: off // size + n]


def build_nc(debug=False):
    nc = bass.Bass("TRN2", target_bir_lowering=False)

    def din(name, shape, dt=F32):
        return nc.dram_tensor(name, shape, dt, kind="ExternalInput").ap()

    xh = din("xh", [T + 128, D])
    w_in = din("w_in", [D, 2304]); w_out = din("w_out", [D, D]); w_up = din("w_up", [D, 4096]); w_down = din("w_down", [4096, D])
    g1 = din("g1", [1, D]); g2 = din("g2", [1, D]); gq = din("gq", [1, 64]); gk = din("gk", [1, 64])
    sinks = din("sinks", [1, 8]); cw = din("cw", [128, 12]); gA = din("gA", [1, 512]); gC = din("gC", [128, 4])
    ident_d = din("ident", [128, 128], BF16); maskg_d = din("maskg", [128, 256], BF16); mask0_d = din("mask0", [128, 256], BF16)
    out = nc.dram_tensor("out", [T, D], F32, kind="ExternalOutput").ap()
    x1s = nc.dram_tensor("x1s", [T, D], F32, kind="ExternalOutput" if debug else "Internal").ap()

    w_in_v = w_in.rearrange("(c p) n -> p c n", p=128)
    w_out_v = w_out.rearrange("(c p) n -> p c n", p=128)
    w_up_v = w_up.rearrange("(c p) n -> p c n", p=128)
    w_down_v = w_down.rearrange("(f p) n -> p f n", p=128)

    P = Prog()
    with ExitStack() as es:
        WA = es.enter_context(nc.sbuf_tensor("WA", [128, 32768], BF16))
        WUPt = es.enter_context(nc.sbuf_tensor("WUP", [128, 32768], BF16))
        ARB = 212863 - 2 * 65536 - 100
        ARB = ARB // 64 * 64
        art = es.enter_context(nc.sbuf_tensor("arena", [128, ARB // 4], F32))
        ps = [es.enter_context(nc.psum_tensor("ps%d" % b, [128, 512], F32)) for b in range(8)]
        ps16 = [p.bitcast(BF16) for p in ps]
        A = Arena(art, ARB)

        Win = WA[:, 0:18432].rearrange("p (c n) -> p c n", c=8)
        Wout = WA[:, 18432:26624].rearrange("p (c n) -> p c n", c=8)
        hT = WA[:, 26624:30720].rearrange("p (c n) -> p c n", c=8)
        qT = [WA[:, 30720 + i * 512: 30720 + (i + 1) * 512].rearrange("p (c n) -> p c n", c=4) for i in range(4)]
        Wd = WA[:, :].rearrange("p (f n) -> p f n", f=32)
        Wup = WUPt[:, :].rearrange("p (c n) -> p c n", c=8)

        ident = A.alloc(BF16, 128)
        ones = A.alloc(BF16, 32)[:, 0:1]
        stat = A.alloc(F32, 256)
        _sc = [0]

        def st(n):
            o = _sc[0]; _sc[0] += n
            assert _sc[0] <= 256
            return stat[:, o:o + n]
        negB = st(1); ones_unused = st(1); esink_h = st(8); esink_s = st(8); sink_t = st(8)
        ss1 = st(4); l1 = st(4); rs1 = st(4)
        ssqk = st(40); lqk = st(40); rqk = st(40)
        den = st(16); rden = st(16)
        ssA = st(4); lA = st(4); rA = st(4)
        lC = st(8); rC = st(8)
        bmax = st(1)
        cwt = st(12); gCt = st(4)
        mark = A.off

        g1t = A.alloc(F32, 1024, ["g1t"])
        xn = [A.alloc(F32, 1024, ["xn%d" % i]) for i in range(2)]
        hb = [A.alloc(BF16, 1024, ["hb%d" % i]) for i in range(NHB)]
        Csb = A.alloc(F32, 512, ["Csb"]); ubuf = A.alloc(F32, 516, ["ubuf"]); acc = A.alloc(F32, 512, ["acc"]); yc = A.alloc(F32, 512, ["yc"])
        sqc = A.alloc(BF16, 512, ["sqc"])
        carry = A.alloc(F32, 8, ["carry%d" % i for i in range(4)]).rearrange("p (c n) -> p c n", c=4)
        sq32 = A.alloc(F32, 640, ["sq32a", "sq32b"]); tmp32 = A.alloc(F32, 640, ["tmp32a", "tmp32b"])
        qkn = [A.alloc(BF16, 768, ["qkn%d" % i]) for i in range(2)]
        gqk = A.alloc(F32, 768, ["gqk"])
        gq_t = A.alloc(F32, 64, ["gq_t"]); gk_t = A.alloc(F32, 64, ["gk_t"]); prod = A.alloc(F32, 64, ["prod"]); prod2 = A.alloc(F32, 64, ["prod2"])
        gAt = A.alloc(F32, 512, ["gAt"])
        maskg = A.alloc(BF16, 256, ["maskg"]); mask0 = A.alloc(BF16, 256, ["mask0"])
        kTr = A.alloc(BF16, 2 * 8 * 128, ["kT%d" % i for i in range(8)]).rearrange("p (k s n) -> p k s n", k=2, s=8)
        vr = A.alloc(BF16, 8 * 2 * 66, ["vr%d" % i for i in range(8)] + ["vr_ones"]).rearrange("p (s k d) -> p s k d", s=8, k=2)
        pT_l = [A.alloc(BF16, 2048, ["pT%d_%d" % (k, i) for i in range(4)]) for k in range(NPT)]
        yattn_l = [A.alloc(F32, 512, ["yattn%d_0" % k, "yattn%d_1" % k]) for k in range(NPT)]
        ya1 = A.alloc(BF16, 512, ["ya"])
        ya = [ya1, ya1]
        yT = A.alloc(BF16, 2048, ["yT%d" % i for i in range(4)]).rearrange("p (c n) -> p c n", c=4)
        ycT = A.alloc(BF16, 2048, ["ycT%d" % i for i in range(4)]).rearrange("p (c n) -> p c n", c=4)
        xr = [A.alloc(F32, 1024, ["xr%d" % i]) for i in range(2)]
        x1t = [A.alloc(F32, 1024, ["x1t%d_0" % i, "x1t%d_1" % i]) for i in range(NX1)]
        recsA = list(A.recs)
        endA = A.off

        import os as _os
        pools = {"conv": [0, 1, 2, 3, 4, 5, 6], "main": [0, 1, 2, 3, 4, 5, 6], "all": [0, 1, 2, 3, 4, 5, 6, 7]}
        if _os.environ.get("KPOOLS"):
            for part in _os.environ["KPOOLS"].split(";"):
                k, v = part.split(":")
                pools[k] = [int(t) for t in v.split(",")]
        if pools["conv"] == pools["main"]:
            bank_shared = True
        else:
            bank_shared = False
        bank_rr = {"conv": 0, "main": 0, "all": 0}

        def nb(pool="main"):
            if pool == "conv" and bank_shared:
                pool = "main"
            lst = pools[pool]
            b = lst[bank_rr[pool] % len(lst)]
            bank_rr[pool] += 1
            return b

        def fsz(ap):
            n = 1
            for d in ap.shape[1:]:
                n *= d
            return n

        def is_ps(ap):
            try:
                return ap.tensor.name.startswith("ps")
            except Exception:
                return False

        def dma(q, out_ap, in_ap, key, reads=(), writes=(), prio=0):
            nbytes = fsz(out_ap) * out_ap.shape[0] * 4
            if q == "sp":
                busy, lat = 120.0, 2500.0 + nbytes / 200.0
                if key.startswith("c_") and key not in ("c_id", "c_g1"):
                    lat = 22000.0
            else:
                busy, lat = 1500.0, 3500.0 + nbytes / 250.0
            db = 0 if (key.startswith("c_") and key not in ("c_id", "c_g1")) else nbytes
            P.add(q, lambda e: e.dma_start(out=out_ap, in_=in_ap), reads=reads, writes=writes, dma=key, busy=busy, lat=lat, prio=prio, dbytes=db)

        def act(out_ap, in_ap, func, reads, writes, **kw):
            d = 230.0 + 0.83 * fsz(in_ap)
            P.add("act", lambda e: e.activation(out=out_ap, in_=in_ap, func=func, **kw), reads=reads, writes=writes, busy=d, lat=d + 60)

        def vdur(eng, out_ap, ins):
            n = fsz(out_ap)
            if eng == "dve":
                per = 1.04
                if out_ap.dtype == BF16 and all(a.dtype == BF16 for a in ins):
                    per = 0.55
                d = 110.0 + per * n + (70.0 if any(is_ps(a) for a in ins) else 0.0)
            else:
                d = 200.0 + 1.9 * n
            return d

        def copy(eng, out_ap, in_ap, reads, writes):
            if eng == "act":
                act(out_ap, in_ap, AF.Copy, reads, writes)
            else:
                d = vdur(eng, out_ap, [in_ap])
                P.add(eng, lambda e: e.tensor_copy(out=out_ap, in_=in_ap), reads=reads, writes=writes, busy=d, lat=d + 60)

        def tt(eng, out_ap, in0, in1, op, reads, writes):
            d = vdur(eng, out_ap, [in0, in1])
            P.add(eng, lambda e: e.tensor_tensor(out=out_ap, in0=in0, in1=in1, op=op), reads=reads, writes=writes, busy=d, lat=d + 60)

        def ts(eng, out_ap, in0, s1, op0, reads, writes, s2=None, op1=None):
            d = vdur(eng, out_ap, [in0])
            if op1 is None:
                P.add(eng, lambda e: e.tensor_scalar(out=out_ap, in0=in0, scalar1=s1, scalar2=None, op0=op0), reads=reads, writes=writes, busy=d, lat=d + 60)
            else:
                P.add(eng, lambda e: e.tensor_scalar(out=out_ap, in0=in0, scalar1=s1, scalar2=s2, op0=op0, op1=op1), reads=reads, writes=writes, busy=d, lat=d + 60)

        def stt(out_ap, in0, scalar, in1, op0, op1, reads, writes):
            d = vdur("dve", out_ap, [in0, in1])
            P.add("dve", lambda e: e.scalar_tensor_tensor(out=out_ap, in0=in0, scalar=scalar, in1=in1, op0=op0, op1=op1), reads=reads, writes=writes, busy=d, lat=d + 60)

        def mm(out_ap, lhsT, rhs, start, stop, reads, writes, **kw):
            n = fsz(rhs)
            if lhsT.dtype == F32:
                d = 135.0
            elif lhsT.shape[0] == 64:
                d = 200.0
            else:
                d = max(62.0, 16.0 + 0.405 * n)
            P.add("pe", lambda e: e.matmul(out_ap, lhsT=lhsT, rhs=rhs, start=start, stop=stop, **kw), reads=reads, writes=writes, busy=d, lat=d + 250)

        def tr(out_ap, in_ap, reads, writes):
            P.add("pe", lambda e: e.transpose(out=out_ap, in_=in_ap, identity=ident), reads=list(reads) + ["ident"], writes=writes, busy=90.0, lat=340.0)

        def rstd(ss_ap, l_ap, r_ap, n, names):
            act(l_ap, ss_ap, AF.Ln, [names[0]], [names[1]], scale=1.0 / n, bias=EPS)
            act(r_ap, l_ap, AF.Exp, [names[1]], [names[2]], scale=-0.5)

        def wload(dst, src, key, name, reads=()):
            dma("pool", dst, src, key, reads=reads, writes=[name])
        for kc in range(8):
            wload(Win[:, kc, :], w_in[kc * 128:(kc + 1) * 128, :], "wik%d" % kc, "Wik%d" % kc)
        dma("sp", ident, ident_d, "c_id", writes=["ident"], prio=-1)
        dma("sp", g1t, g1.partition_broadcast(128), "c_g1", writes=["g1t"], prio=-1)
        dma("sp", gq_t, gq.partition_broadcast(128), "c_gq", writes=["gq_t"])
        dma("sp", gk_t, gk.partition_broadcast(128), "c_gk", writes=["gk_t"])
        dma("sp", sink_t, sinks.partition_broadcast(128), "c_sk", writes=["sink_t"])
        dma("sp", cwt, cw, "c_cw", writes=["cwt"])
        dma("sp", gCt, gC, "c_gC", writes=["gCt"])
        dma("sp", gAt, gA.partition_broadcast(128), "c_gA", writes=["gAt"])
        dma("sp", maskg, maskg_d, "c_mg", writes=["maskg"])
        dma("sp", mask0, mask0_d, "c_m0", writes=["mask0"])
        wload(Wout[:, :, :], w_out_v[:, :, :], "wo", "Wout")
        for j in range(8):
            wload(Wup[:, :, j * 512:(j + 1) * 512], w_up_v[:, :, j * 512:(j + 1) * 512], "wu%d" % j, "Wup%d" % j, reads=["x1s%d" % (1 + 4 * (j // 2))])

        P.add("pool", lambda e: e.memset(ones, 1.0), writes=["ones"])
        P.add("pool", lambda e: e.memset(vr[:, :, :, 64:65], 1.0), writes=["vr_ones"])
        cw3 = cwt.rearrange("p (c j) -> p c j", c=4)
        gqk3 = gqk.rearrange("p (h d) -> p h d", d=64)
        copy("dve", gqk3[:, 0:8, :], bc(gq_t, 1, 8), ["gq_t"], ["gqk"])
        ts("dve", gqk3[:, 8:12, :], bc(gk_t, 1, 4), 0.125, ALU.mult, ["gk_t"], ["gqk"])
        tt("dve", prod, gq_t, gk_t, ALU.mult, ["gq_t", "gk_t"], ["prod"])
        ts("dve", prod2, prod, -1.0, ALU.mult, ["prod"], ["prod2"])
        tt("dve", prod, prod, prod2, ALU.max, ["prod", "prod2"], ["prod"])
        P.add("dve", lambda e: e.tensor_reduce(out=bmax, in_=prod, axis=AX.X, op=ALU.max), reads=["prod"], writes=["bmax"])
        ts("dve", negB, bmax, -8.0, ALU.mult, ["bmax"], ["negB"])
        act(esink_h, sink_t, AF.Exp, ["sink_t", "negB"], ["esink_h"], bias=negB, scale=1.0)
        copy("dve", esink_s.rearrange("p (k f c) -> p k f c", k=2, f=2), esink_h.rearrange("p (k c f) -> p k f c", k=2, c=2), ["esink_h"], ["esink_s"])

        def tiles_of(m):
            if m < 0:
                return [(0, 0)]
            return [(s, 1 + 4 * m + s) for s in range(4)]

        def NTs(m):
            P.tag = 'NT%d' % m
            for (s, ti) in tiles_of(m):
                sl = ti % 2; q4 = ti % 4; hs = ti % NHB
                col = s * 128
                dma("sp", xn[sl], xh[ti * 128:(ti + 1) * 128, :], "xn%d" % sl, writes=["xn%d" % sl], prio=-1)
                act(hb[hs], xn[sl], AF.Square, ["xn%d" % sl], ["hb%d" % hs, "ss1_%d" % q4], accum_out=ss1[:, q4:q4 + 1])
                rstd(ss1[:, q4:q4 + 1], l1[:, q4:q4 + 1], rs1[:, q4:q4 + 1], D, ("ss1_%d" % q4, "l1_%d" % q4, "rs1_%d" % q4))
                stt(hb[hs], xn[sl], rs1[:, q4:q4 + 1], g1t, ALU.mult, ALU.mult, ["xn%d" % sl, "rs1_%d" % q4, "g1t"], ["hb%d" % hs])
                b = nb()
                for c in range(8):
                    tr(ps16[b][:, c * 128:(c + 1) * 128], hb[hs][:, c * 128:(c + 1) * 128], ["hb%d" % hs], ["ps%d" % b])
                copy("act", hT[:, :, col:col + 128], ps16[b][:, :].rearrange("p (c n) -> p c n", c=8), ["ps%d" % b], ["hT%d" % s])

        def Zs(m):
            P.tag = 'Z%d' % m
            for (s, ti) in tiles_of(m):
                sl = ti % 2; q4 = ti % 4; s8 = ti % 8
                col = s * 128
                bq = nb()
                for kc in range(8):
                    mm(ps[bq][:, :], hT[:, kc, col:col + 128], Win[:, kc, 0:512], kc == 0, kc == 7, ["hT%d" % s, "Wik%d" % kc], ["ps%d" % bq])
                bk = nb()
                for kc in range(8):
                    mm(ps[bk][:, 0:256], hT[:, kc, col:col + 128], Win[:, kc, 512:768], kc == 0, kc == 7, ["hT%d" % s, "Wik%d" % kc], ["ps%d" % bk])
                act(sq32[:, 0:512], ps[bq][:, :], AF.Square, ["ps%d" % bq], ["sq32a"])
                act(sq32[:, 512:640], ps[bk][:, 0:128], AF.Square, ["ps%d" % bk], ["sq32b"])
                sv = ssqk[:, q4 * 10:(q4 + 1) * 10]; lv = lqk[:, q4 * 10:(q4 + 1) * 10]; rv = rqk[:, q4 * 10:(q4 + 1) * 10]
                P.add("dve", lambda e, sv=sv: e.tensor_reduce(out=sv, in_=sq32.rearrange("p (h d) -> p h d", d=64), axis=AX.X, op=ALU.add),
                      reads=["sq32a", "sq32b"], writes=["ssqk%d" % q4], busy=780.0, lat=840.0)
                rstd(sv, lv, rv, 64, ("ssqk%d" % q4, "lqk%d" % q4, "rqk%d" % q4))
                tt("dve", tmp32[:, 0:512].rearrange("p (h d) -> p h d", d=64), ps[bq][:, :].rearrange("p (h d) -> p h d", d=64),
                   bc(rv[:, 0:8], 2, 64), ALU.mult, ["ps%d" % bq, "rqk%d" % q4], ["tmp32a"])
                tt("dve", tmp32[:, 512:640].rearrange("p (h d) -> p h d", d=64), ps[bk][:, 0:128].rearrange("p (h d) -> p h d", d=64),
                   bc(rv[:, 8:10], 2, 64), ALU.mult, ["ps%d" % bk, "rqk%d" % q4], ["tmp32b"])
                qn = qkn[sl]
                tt("pool", qn[:, 0:512], tmp32[:, 0:512], gqk[:, 0:512], ALU.mult, ["tmp32a", "gqk"], ["qkn%d" % sl])
                tt("pool", qn[:, 512:768].rearrange("p (k u d) -> p k u d", k=2, u=2),
                   bc(tmp32[:, 512:640].rearrange("p (k d) -> p k d", k=2), 2, 2),
                   gqk[:, 512:768].rearrange("p (k u d) -> p k u d", k=2, u=2), ALU.mult, ["tmp32b", "gqk"], ["qkn%d" % sl])
                copy("act", vr[:, s8, :, 0:64], ps[bk][:, 128:256].rearrange("p (k d) -> p k d", k=2), ["ps%d" % bk], ["vr%d" % s8])
                bt = nb()
                for c in range(6):
                    tr(ps16[bt][:, c * 128:(c + 1) * 128], qn[:, c * 128:(c + 1) * 128], ["qkn%d" % sl], ["ps%d" % bt])
                copy("dve", qT[s], ps16[bt][:, 0:512].rearrange("p (c n) -> p c n", c=4), ["ps%d" % bt], ["qT%d" % s])
                copy("dve", kTr[:, :, s8, :], ps16[bt][:, 512:768].rearrange("p (k n) -> p k n", k=2), ["ps%d" % bt], ["kT%d" % s8])

        def ATTs(m):
            P.tag = 'ATT%d' % m
            for (s, ti) in tiles_of(m):
                sl = ti % 2; q4 = ti % 4
                col = s * 128
                kts = [(ti - 1) % 8, ti % 8]
                pi = ti % NPT
                pT = pT_l[pi]; yattn = yattn_l[pi]
                pT5 = pT.rearrange("p (j b c q) -> p j b c q", j=2, b=4, c=2)
                pTn = ["pT%d_%d" % (pi, b_) for b_ in range(4)]
                yan = ["yattn%d_%d" % (pi, k_) for k_ in range(2)]
                for kv in range(2):
                    bSs = [nb(), nb()]
                    for j in range(2):
                        for half in range(2):
                            bS = bSs[half]
                            lo = half * 64
                            mm(ps[bS][:, j * 256:(j + 1) * 256].rearrange("p (c n) -> p c n", c=2),
                               kTr[lo:lo + 64, kv, kts[j], :], qT[s][lo:lo + 64, 2 * kv:2 * kv + 2, :], True, True,
                               ["kT%d" % kts[j], "qT%d" % s], ["ps%d" % bS])
                    for half in range(2):
                        b4 = kv * 2 + half
                        bS = bSs[half]
                        act(pT5[:, :, b4, :, :], ps[bS][:, :].rearrange("p (j c q) -> p j c q", j=2, c=2), AF.Exp,
                            ["ps%d" % bS, "negB"], [pTn[b4]], bias=negB, scale=1.0)
                mk = mask0 if ti == 1 else maskg
                mkn = "mask0" if ti == 1 else "maskg"
                pT4 = pT.rearrange("p (j r q) -> p j r q", j=2, r=8)
                for kv in range(2):
                    pv = pT4[:, :, 4 * kv:4 * kv + 4, :]
                    tt(MASK_ENG, pv, pv, bc(mk.rearrange("p (j q) -> p j q", j=2), 2, 4), ALU.mult,
                       [pTn[2 * kv], pTn[2 * kv + 1], mkn], [pTn[2 * kv], pTn[2 * kv + 1]])
                d8 = den[:, sl * 8:(sl + 1) * 8]; r8 = rden[:, sl * 8:(sl + 1) * 8]
                bOs = []
                for kv in range(2):
                    bO = nb(); bOs.append(bO)
                    for half in range(2):
                        for c in range(2):
                            s4 = half * 2 + c; b4 = kv * 2 + half
                            for j in range(2):
                                mm(ps[bO][:, s4 * 128:s4 * 128 + 65], pT5[:, j, b4, c, :], vr[:, kts[j], kv, 0:65], j == 0, j == 1,
                                   [pTn[b4], "vr%d" % kts[j], "vr_ones"], ["ps%d" % bO])
                    o3v = ps[bO][:, :].rearrange("p (s x) -> p s x", s=4)
                    tt("dve", d8[:, kv * 4:kv * 4 + 4], o3v[:, :, 64], esink_s[:, kv * 4:kv * 4 + 4], ALU.add, ["ps%d" % bO, "esink_s"], ["den%d_%d" % (sl, kv)])
                P.add("dve", lambda e, d8=d8, r8=r8: e.reciprocal(out=r8, in_=d8), reads=["den%d_0" % sl, "den%d_1" % sl], writes=["rden%d" % sl])
                for kv in range(2):
                    bO = bOs[kv]
                    tt("dve", yattn[:, kv * 256:(kv + 1) * 256].rearrange("p (c f d) -> p f c d", c=2, f=2),
                       ps[bO][:, :].rearrange("p (f c x) -> p f c x", f=2, c=2)[:, :, :, 0:64],
                       bc(r8[:, kv * 4:kv * 4 + 4].rearrange("p (f c) -> p f c", f=2), 3, 64), ALU.mult,
                       ["ps%d" % bO, "rden%d" % sl], [yan[kv]])
                act(tmp32[:, 0:512], yattn, AF.Square, yan, ["tmp32a", "ssA%d" % q4], accum_out=ssA[:, q4:q4 + 1])
                rstd(ssA[:, q4:q4 + 1], lA[:, q4:q4 + 1], rA[:, q4:q4 + 1], 512, ("ssA%d" % q4, "lA%d" % q4, "rA%d" % q4))
                stt(ya[sl], yattn, rA[:, q4:q4 + 1], gAt, ALU.mult, ALU.mult, yan + ["rA%d" % q4, "gAt"], ["ya"])
                bt = nb()
                for c in range(4):
                    tr(ps16[bt][:, c * 128:(c + 1) * 128], ya[sl][:, c * 128:(c + 1) * 128], ["ya"], ["ps%d" % bt])
                copy("act", yT[:, :, col:col + 128], ps16[bt][:, 0:512].rearrange("p (c n) -> p c n", c=4), ["ps%d" % bt], ["yT%d" % s])

        def CONVs(m):
            P.tag = 'CONV%d' % m
            halo = m < 0
            N = 128 if halo else 512
            hts = ["hT0"] if halo else ["hT0", "hT1", "hT2", "hT3"]
            m2 = m % 2
            for i in range(4):
                def grp(coff, wname):
                    b = nb("conv")
                    for kc in range(8):
                        mm(ps[b][:, 0:N], Win[:, kc, coff + 128 * i: coff + 128 * (i + 1)], hT[:, kc, 0:N], kc == 0, kc == 7, hts + ["Wik%d" % kc], ["ps%d" % b])
                    return b
                bC = grp(O3, "Win2")
                bX = grp(O4, "Win3")
                copy("act", Csb[:, 0:N], ps[bC][:, 0:N], ["ps%d" % bC], ["Csb"])
                if not halo:
                    copy("pool", ubuf[:, 0:2], carry[:, i, :], ["carry%d" % i], ["ubuf"])
                tt("dve", ubuf[:, 2:2 + N], Csb[:, 0:N], ps[bX][:, 0:N], ALU.mult, ["Csb", "ps%d" % bX], ["ubuf"])
                copy("pool", carry[:, i, :], ubuf[:, N:N + 2], ["ubuf"], ["carry%d" % i])
                if halo:
                    continue
                bB = grp(O2, "Win1")
                ts("dve", acc, ubuf[:, 2:514], cw3[:, i, 2:3], ALU.mult, ["ubuf", "cwt"], ["acc"])
                stt(acc, ubuf[:, 1:513], cw3[:, i, 1:2], acc, ALU.mult, ALU.add, ["ubuf", "cwt", "acc"], ["acc"])
                stt(acc, ubuf[:, 0:512], cw3[:, i, 0:1], acc, ALU.mult, ALU.add, ["ubuf", "cwt", "acc"], ["acc"])
                tt("dve", yc, acc, ps[bB][:, :], ALU.mult, ["acc", "ps%d" % bB], ["yc"])
                act(sqc, yc, AF.Square, ["yc"], ["sqc"])
                for s in range(4):
                    mm(ps[7][:, s:s + 1], sqc[:, s * 128:(s + 1) * 128], ones, (i == 0 and s == 0), (i == 3 and s == 3), ["sqc", "ones"], ["ps7"], skip_group_check=True)
                act(ycT[:, i, :], yc, AF.Identity, ["yc", "gCt"], ["ycT%d" % i], scale=gCt[:, i:i + 1])
            if not halo:
                lv = lC[:, m2 * 4:m2 * 4 + 4]; rv = rC[:, m2 * 4:m2 * 4 + 4]
                act(lv, ps[7][:, 0:4], AF.Ln, ["ps7"], ["lC%d" % m2], scale=1.0 / 512, bias=EPS)
                act(rv, lv, AF.Exp, ["lC%d" % m2], ["rC%d" % m2], scale=-0.5)

        def OUTs(m):
            P.tag = 'OUT%d' % m
            m2 = m % 2
            for (s, ti) in tiles_of(m):
                sl = ti % 2
                col = s * 128
                dma("sp", xr[sl], xh[ti * 128:(ti + 1) * 128, :], "xr%d" % sl, writes=["xr%d" % sl])
                for n in range(2):
                    bA = nb()
                    for c in range(4):
                        mm(ps[bA][:, :], yT[:, c, col:col + 128], Wout[:, c, n * 512:(n + 1) * 512], c == 0, c == 3, ["yT%d" % s, "Wout"], ["ps%d" % bA])
                    bCc = nb()
                    for c in range(4):
                        mm(ps[bCc][:, :], ycT[:, c, col:col + 128], Wout[:, 4 + c, n * 512:(n + 1) * 512], c == 0, c == 3, ["ycT%d" % c, "Wout"], ["ps%d" % bCc])
                    x1 = ti % NX1
                    xs = x1t[x1][:, n * 512:(n + 1) * 512]
                    stt(xs, ps[bCc][:, :], rC[:, m2 * 4 + s:m2 * 4 + s + 1], xr[sl][:, n * 512:(n + 1) * 512], ALU.mult, ALU.add,
                        ["ps%d" % bCc, "rC%d" % m2, "xr%d" % sl], ["x1t%d_%d" % (x1, n)])
                    tt("dve", xs, xs, ps[bA][:, :], ALU.add, ["x1t%d_%d" % (x1, n), "ps%d" % bA], ["x1t%d_%d" % (x1, n)])
                x1 = ti % NX1
                dma("sp", x1s[(ti - 1) * 128:ti * 128, :], x1t[x1], "x1st%d" % x1, reads=["x1t%d_0" % x1, "x1t%d_1" % x1], writes=["x1s%d" % ti])

        NTs(-1); Zs(-1); CONVs(-1)
        NTs(0)
        for m in range(NM):
            Zs(m)
            CONVs(m)
            ATTs(m)
            if m + 1 < NM:
                NTs(m + 1)
            OUTs(m)

        A.off = mark
        nA = len(A.recs)
        g2t = A.alloc(F32, 1024, ["g2t"])
        xn2 = [A.alloc(F32, 1024, ["xn2_%d" % i]) for i in range(2)]
        hm = [A.alloc(BF16, 1024, ["hm%d" % i]) for i in range(2)]
        hmT = [None, None]
        hmT[0] = A.alloc(BF16, 4096, ["hmT0_%d" % i for i in range(4)]).rearrange("p (c n) -> p c n", c=8)
        r32 = [A.alloc(F32, 512, ["r32_%d" % i]) for i in range(2)]
        hmT[1] = A.alloc(BF16, 4096, ["hmT1_%d" % i for i in range(4)]).rearrange("p (c n) -> p c n", c=8)
        xr2 = [A.alloc(F32, 1024, ["xr2_%d_0" % i, "xr2_%d_1" % i]) for i in range(2)]
        actT = A.alloc(BF16, 32 * 512, ["actT%d" % i for i in range(32)]).rearrange("p (f n) -> p f n", f=32)
        ss2 = st(4); l2 = st(4); rs2 = st(4)
        for (b0, b1, bn) in A.recs[nA:]:
            for (a0, a1, an) in recsA:
                if a0 < b1 and b0 < a1:
                    for nm in bn:
                        P.alias.setdefault(nm, []).extend(an)
        wa_in = ["Wik%d" % k_ for k_ in range(8)]
        wa_out = ["Wout"]
        wa_sp = ["hT%d" % i for i in range(4)] + ["qT%d" % i for i in range(4)]
        P.alias["Wd0"] = list(wa_in); P.alias["Wd1"] = list(wa_in); P.alias["Wd2"] = wa_in + wa_out; P.alias["Wd3"] = wa_out + wa_sp

        for j in range(4):
            dma("pool", Wd[:, 8 * j:8 * j + 8, :], w_down_v[:, 8 * j:8 * j + 8, :], "wd%d" % j, writes=["Wd%d" % j])
        dma("sp", g2t, g2.partition_broadcast(128), "c_g2", writes=["g2t"])

        def NT2(m):
            P.tag = 'NT2%d' % m
            mb = m % 2
            for (s, ti) in tiles_of(m):
                sl = ti % 2; q4 = ti % 4
                col = s * 128
                dma("sp", xn2[sl], x1s[(ti - 1) * 128:ti * 128, :], "xn2_%d" % sl, reads=["x1s%d" % ti], writes=["xn2_%d" % sl])
                act(hm[sl], xn2[sl], AF.Square, ["xn2_%d" % sl], ["hm%d" % sl, "ss2_%d" % q4], accum_out=ss2[:, q4:q4 + 1])
                rstd(ss2[:, q4:q4 + 1], l2[:, q4:q4 + 1], rs2[:, q4:q4 + 1], D, ("ss2_%d" % q4, "l2_%d" % q4, "rs2_%d" % q4))
                stt(hm[sl], xn2[sl], rs2[:, q4:q4 + 1], g2t, ALU.mult, ALU.mult, ["xn2_%d" % sl, "rs2_%d" % q4, "g2t"], ["hm%d" % sl])
                b = nb("all")
                for c in range(8):
                    tr(ps16[b][:, c * 128:(c + 1) * 128], hm[sl][:, c * 128:(c + 1) * 128], ["hm%d" % sl], ["ps%d" % b])
                copy("act", hmT[mb][:, :, col:col + 128], ps16[b][:, :].rearrange("p (c n) -> p c n", c=8), ["ps%d" % b], ["hmT%d_%d" % (mb, s)])

        def UP(m):
            P.tag = 'UP%d' % m
            mb = m % 2
            hn = ["hmT%d_%d" % (mb, s) for s in range(4)]
            for f in range(32):
                b = nb("all")
                for kc in range(8):
                    mm(ps[b][:, :], Wup[:, kc, f * 128:(f + 1) * 128], hmT[mb][:, kc, :], kc == 0, kc == 7, hn + ["Wup%d" % (f // 4)], ["ps%d" % b])
                r = r32[f % 2]
                act(r, ps[b][:, :], AF.Relu, ["ps%d" % b], ["r32_%d" % (f % 2)])
                tt("dve" if f % 2 == 0 else "pool", actT[:, f, :], r, r, ALU.mult, ["r32_%d" % (f % 2)], ["actT%d" % f])

        def DOWN(m):
            P.tag = 'DOWN%d' % m
            for (s, ti) in tiles_of(m):
                sl = ti % 2
                col = s * 128
                dma("sp", xr2[sl], x1s[(ti - 1) * 128:ti * 128, :], "xr2_%d" % sl, reads=["x1s%d" % ti], writes=["xr2_%d_0" % sl, "xr2_%d_1" % sl])
                for n in range(2):
                    b = nb("all")
                    for f in range(32):
                        mm(ps[b][:, :], actT[:, f, col:col + 128], Wd[:, f, n * 512:(n + 1) * 512], f == 0, f == 31, ["actT%d" % f, "Wd%d" % (f // 8)], ["ps%d" % b])
                    xs = xr2[sl][:, n * 512:(n + 1) * 512]
                    tt("dve", xs, xs, ps[b][:, :], ALU.add, ["xr2_%d_%d" % (sl, n), "ps%d" % b], ["xr2_%d_%d" % (sl, n)])
                dma("sp", out[(ti - 1) * 128:ti * 128, :], xr2[sl], "ost%d" % sl, reads=["xr2_%d_0" % sl, "xr2_%d_1" % sl])

        NT2(0)
        for m in range(NM):
            UP(m)
            if m + 1 < NM:
                NT2(m + 1)
            DOWN(m)

        P.finalize()
        sems = {}
        for k in P.dma_counts:
            sems[("dma", k)] = es.enter_context(nc.semaphore("d_" + k))
        for k in ENGS:
            sems[("eng", k)] = es.enter_context(nc.semaphore("e_" + k))
        with nc.Block() as block:
            @block.sync
            def _(e):
                P.emit("sp", e, sems, final_waits=[("dma", "ost0"), ("dma", "ost1")])

            @block.gpsimd
            def _(e):
                P.emit("pool", e, sems)

            @block.scalar
            def _(e):
                P.emit("act", e, sems)

            @block.vector
            def _(e):
                P.emit("dve", e, sems)

            @block.tensor
            def _(e):
                P.emit("pe", e, sems)
    return nc


_NC_CACHE = {}


def prep_inputs(x, attn_norm_g, w_in, q_norm_g, k_norm_g, sinks, conv_w, attn_out_g, conv_out_g, w_out, mlp_norm_g, w_up, w_down):
    f = lambda a: np.ascontiguousarray(np.asarray(a, dtype=np.float32))
    x = f(x)
    B, S, _ = x.shape
    assert (B, S) == (4, 8192)
    bf = ml_dtypes.bfloat16
    kk = np.arange(128)[:, None]; qq = np.arange(128)[None, :]
    m_prev = (kk > qq).astype(np.float32); m_cur = (kk <= qq).astype(np.float32)
    maskg = np.concatenate([m_prev, m_cur], axis=1).astype(bf)
    mask0_first = np.concatenate([np.zeros_like(m_prev), m_cur], axis=1).astype(bf)
    ident = np.eye(128, dtype=np.float32).astype(bf)
    common = dict(
        w_in=f(w_in[0]), w_out=f(w_out[0]), w_up=f(w_up[0]), w_down=f(w_down[0]),
        g1=f(attn_norm_g[0]).reshape(1, D), g2=f(mlp_norm_g[0]).reshape(1, D),
        gq=f(q_norm_g[0]).reshape(1, 64), gk=f(k_norm_g[0]).reshape(1, 64), sinks=f(sinks[0]).reshape(1, 8),
        cw=np.ascontiguousarray(f(conv_w[0]).reshape(3, 4, 128).transpose(2, 1, 0).reshape(128, 12)),
        gA=f(attn_out_g[0]).reshape(1, 512),
        gC=np.ascontiguousarray(f(conv_out_g[0]).reshape(4, 128).T),
        ident=ident, maskg=maskg,
    )
    in_maps = []
    for c in range(8):
        b, half = c // 2, c % 2
        if half == 0:
            xh = np.concatenate([np.zeros((128, D), np.float32), x[b, 0:T]], axis=0)
            m0 = mask0_first
        else:
            xh = x[b, T - 128:2 * T]
            m0 = maskg
        d = dict(common)
        d["xh"] = np.ascontiguousarray(xh)
        d["mask0"] = m0
        in_maps.append(d)
    return in_maps


def kernel(**inputs):
    in_maps = prep_inputs(**inputs)
    B, S = 4, 8192
    if "nc" not in _NC_CACHE:
        _NC_CACHE["nc"] = build_nc()
    nc = _NC_CACHE["nc"]
    res = run_bass_kernel_spmd(nc, in_maps, core_ids=list(range(8)))
    outp = np.empty((B, S, D), np.float32)
    for c in range(8):
        b, half = c // 2, c % 2
        outp[b, half * T:(half + 1) * T] = res.results[c]["out"]
    return outp
```

```python
import numpy as np
import ml_dtypes
from contextlib import ExitStack
import concourse.bass as bass
import concourse.mybir as mybir
from concourse.bass_utils import run_bass_kernel_spmd

F32 = mybir.dt.float32
BF16 = mybir.dt.bfloat16
AF = mybir.ActivationFunctionType
ALU = mybir.AluOpType
AX = mybir.AxisListType

D = 1024
T = 4096
NTILE = T // 128
NM = T // 512
O1, O2, O3, O4 = 640, 768, 1280, 1792
EPS = 1e-6
import os as _os0
ENGS = ["pe", "act", "dve", "pool", "sp"]
MASK_ENG = _os0.environ.get("KMASK", "dve")
NPT = int(_os0.environ.get("KNPT", "1"))
NX1 = int(_os0.environ.get("KNX1", "2"))
NHB = int(_os0.environ.get("KNHB", "2"))


class Prog:
    XLAT = float(_os0.environ.get('KXLAT', '700'))
    WIN = float(_os0.environ.get('KWIN', '100'))
    USE_BL = int(_os0.environ.get('KBL', '1'))
    SE_ALL = int(_os0.environ.get('KSEALL', '1'))

    def __init__(self, same_engine_raw=True, do_schedule=True):
        self.ops = []
        self.buf = {}
        self.same_engine_raw = same_engine_raw
        self.do_schedule = do_schedule
        self.last_on_eng = {}
        self.last_dma = {}
        self.pending_bar = {}
        self.bar_first = {}
        self.alias = {}

    def add(self, eng, fn, reads=(), writes=(), dma=None, busy=200.0, lat=None, prio=0):
        i = len(self.ops)
        op = dict(eng=eng, fn=fn, deps={}, dma=dma, sig=False, idx=i, busy=float(busy), lat=float(busy if lat is None else lat), tag=getattr(self, 'tag', ''), prio=prio)
        deps = op["deps"]

        def dep(j, kind):
            if j is None or j == i:
                return
            if deps.get(j) != "raw":
                deps[j] = kind
        for r in reads:
            st = self.buf.setdefault(r, dict(w=None, r=[]))
            dep(st["w"], "raw")
        for w in writes:
            st = self.buf.setdefault(w, dict(w=None, r=[]))
            dep(st["w"], "waw")
            for rr in st["r"]:
                dep(rr, "war")
        for r in reads:
            self.buf[r]["r"].append(i)
        for w in writes:
            st = self.buf[w]
            st["w"] = i
            st["r"] = []
        for nm in list(reads) + list(writes):
            if nm in self.alias:
                for a in self.alias.pop(nm):
                    st = self.buf.get(a)
                    if st is not None:
                        dep(st["w"], "waw")
                        for rr in st["r"]:
                            dep(rr, "war")
        if eng in self.pending_bar:
            for j in self.pending_bar.pop(eng):
                dep(j, "bar")
            self.bar_first[eng] = i
        elif eng in self.bar_first:
            dep(self.bar_first[eng], "bar")
        if dma is not None:
            self.last_dma[dma] = i
        else:
            self.last_on_eng[eng] = i
        self.ops.append(op)
        return i

    def barrier(self):
        deps = list(self.last_on_eng.values()) + list(self.last_dma.values())
        for e in ENGS:
            self.pending_bar[e] = list(deps)

    def schedule(self):
        ops = self.ops
        n = len(ops)
        if not self.do_schedule:
            self.order = list(range(n))
            return
        succ = [[] for _ in range(n)]
        nleft = [0] * n
        for op in ops:
            nleft[op["idx"]] = len(op["deps"])
            for j in op["deps"]:
                succ[j].append(op["idx"])
        blevel = [0.0] * n
        for i in range(n - 1, -1, -1):
            b = blevel[i] + ops[i]["lat"]
            ops[i]["blevel"] = b
            for j in ops[i]["deps"]:
                if b > blevel[j]:
                    blevel[j] = b
        W = self.WIN
        tfree = {e: 0.0 for e in ENGS}
        avail = [0.0] * n
        dready = [0.0] * n
        ready = {e: [] for e in ENGS}
        for op in ops:
            if nleft[op["idx"]] == 0:
                ready[op["eng"]].append(op["idx"])
        order = []
        done = 0
        while done < n:
            best = None
            cands = []
            for e in ENGS:
                tf = tfree[e]
                for i in ready[e]:
                    t = dready[i] if dready[i] > tf else tf
                    cands.append((t, i))
                    if best is None or t < best:
                        best = t
            bk = None
            for (t, i) in cands:
                if t <= best + W:
                    key = (ops[i]["prio"], -ops[i]["blevel"] if self.USE_BL else 0, t, i)
                    if bk is None or key < bk:
                        bk = key
                        bt = t
            i = bk[3]
            t = bt
            op = ops[i]
            e = op["eng"]
            ready[e].remove(i)
            tfree[e] = t + op["busy"]
            avail[i] = t + op["lat"]
            op["t_start"] = t
            order.append(i)
            done += 1
            for k in succ[i]:
                nleft[k] -= 1
                if ops[k]["eng"] == e and op["dma"] is None and ops[k]["dma"] is None:
                    if e != "pe" and (ops[k]["deps"][i] == "raw" or self.SE_ALL) and self.same_engine_raw:
                        a = avail[i]
                    else:
                        a = t + op["busy"]
                else:
                    a = avail[i] + self.XLAT
                if a > dready[k]:
                    dready[k] = a
                    ops[k]["crit"] = i
                if nleft[k] == 0:
                    ready[ops[k]["eng"]].append(k)
        self.order = order
        self.est_total = max(avail)

    def finalize(self):
        self.schedule()
        ops = self.ops
        pos = {i: p for p, i in enumerate(self.order)}
        self.dma_counts = {}
        for i in self.order:
            op = ops[i]
            if op["dma"] is not None:
                self.dma_counts[op["dma"]] = self.dma_counts.get(op["dma"], 0) + 16
                op["dma_val"] = self.dma_counts[op["dma"]]
        for op in ops:
            nd = []
            for j, kind in op["deps"].items():
                p = ops[j]
                assert pos[j] < pos[op["idx"]]
                if p["dma"] is None and op["dma"] is None and p["eng"] == op["eng"]:
                    if op["eng"] == "pe":
                        continue
                    if not ((kind == "raw" or self.SE_ALL) and self.same_engine_raw):
                        continue
                nd.append(j)
            latest = {}
            keep = []
            for j in nd:
                p = ops[j]
                if p["dma"] is not None:
                    keep.append(j)
                else:
                    if p["eng"] not in latest or pos[j] > pos[latest[p["eng"]]]:
                        latest[p["eng"]] = j
            keep += list(latest.values())
            op["xdeps"] = keep
            for j in keep:
                ops[j]["sig"] = True
        cnt = {}
        for i in self.order:
            op = ops[i]
            if op["dma"] is None and op["sig"]:
                cnt[op["eng"]] = cnt.get(op["eng"], 0) + 1
                op["val"] = cnt[op["eng"]]

    def emit(self, engname, engobj, sems, final_waits=()):
        waited = {}
        ops = self.ops
        for i in self.order:
            op = ops[i]
            if op["eng"] != engname:
                continue
            need = {}
            for j in op["xdeps"]:
                p = ops[j]
                if p["dma"] is not None:
                    key = ("dma", p["dma"]); val = p["dma_val"]
                else:
                    key = ("eng", p["eng"]); val = p["val"]
                if val > need.get(key, 0):
                    need[key] = val
            for key, val in need.items():
                if waited.get(key, 0) >= val:
                    continue
                engobj.wait_ge(sems[key], val)
                waited[key] = val
            ins = op["fn"](engobj)
            if op["dma"] is not None:
                ins.then_inc(sems[("dma", op["dma"])], 16)
            elif op["sig"]:
                ins.then_inc(sems[("eng", op["eng"])], 1)
        for key in final_waits:
            engobj.wait_ge(sems[key], self.dma_counts[key[1]])


def bc(ap, axis, n):
    l = [list(x) for x in ap.ap]
    l.insert(axis, [0, n])
    return bass.AP(ap.tensor, ap.offset, l)


class Arena:
    def __init__(self, t32, nbytes):
        self.t32 = t32
        self.t16 = t32.bitcast(BF16)
        self.cap = nbytes
        self.off = 0
        self.recs = []

    def alloc(self, dt, n, names=()):
        size = 4 if dt == F32 else 2
        off = (self.off + 63) // 64 * 64
        assert off + n * size <= self.cap, ("arena overflow", off, n * size, self.cap)
        self.off = off + n * size
        self.recs.append((off, off + n * size, list(names)))
        t = self.t32 if dt == F32 else self.t16
        return t[:, off // size: off // size + n]


def build_nc(debug=False):
    nc = bass.Bass("TRN2", target_bir_lowering=False)

    def din(name, shape, dt=F32):
        return nc.dram_tensor(name, shape, dt, kind="ExternalInput").ap()

    xh = din("xh", [T + 128, D])
    w_in = din("w_in", [D, 2304]); w_out = din("w_out", [D, D]); w_up = din("w_up", [D, 4096]); w_down = din("w_down", [4096, D])
    g1 = din("g1", [1, D]); g2 = din("g2", [1, D]); gq = din("gq", [1, 64]); gk = din("gk", [1, 64])
    sinks = din("sinks", [1, 8]); cw = din("cw", [128, 12]); gA = din("gA", [1, 512]); gC = din("gC", [128, 4])
    ident_d = din("ident", [128, 128], BF16); maskg_d = din("maskg", [128, 256], BF16); mask0_d = din("mask0", [128, 256], BF16)
    out = nc.dram_tensor("out", [T, D], F32, kind="ExternalOutput").ap()
    x1s = nc.dram_tensor("x1s", [T, D], F32, kind="ExternalOutput" if debug else "Internal").ap()

    w_in_v = w_in.rearrange("(c p) n -> p c n", p=128)
    w_out_v = w_out.rearrange("(c p) n -> p c n", p=128)
    w_up_v = w_up.rearrange("(c p) n -> p c n", p=128)
    w_down_v = w_down.rearrange("(f p) n -> p f n", p=128)

    P = Prog()
    with ExitStack() as es:
        WA = es.enter_context(nc.sbuf_tensor("WA", [128, 32768], BF16))
        WUPt = es.enter_context(nc.sbuf_tensor("WUP", [128, 32768], BF16))
        ARB = 212863 - 2 * 65536 - 100
        ARB = ARB // 64 * 64
        art = es.enter_context(nc.sbuf_tensor("arena", [128, ARB // 4], F32))
        ps = [es.enter_context(nc.psum_tensor("ps%d" % b, [128, 512], F32)) for b in range(8)]
        ps16 = [p.bitcast(BF16) for p in ps]
        A = Arena(art, ARB)

        Win = WA[:, 0:18432].rearrange("p (c n) -> p c n", c=8)
        Wout = WA[:, 18432:26624].rearrange("p (c n) -> p c n", c=8)
        hT = WA[:, 26624:30720].rearrange("p (c n) -> p c n", c=8)
        qT = [WA[:, 30720 + i * 512: 30720 + (i + 1) * 512].rearrange("p (c n) -> p c n", c=4) for i in range(4)]
        Wd = WA[:, :].rearrange("p (f n) -> p f n", f=32)
        Wup = WUPt[:, :].rearrange("p (c n) -> p c n", c=8)

        ident = A.alloc(BF16, 128)
        ones = A.alloc(BF16, 32)[:, 0:1]
        stat = A.alloc(F32, 256)
        _sc = [0]

        def st(n):
            o = _sc[0]; _sc[0] += n
            assert _sc[0] <= 256
            return stat[:, o:o + n]
        negB = st(1); ones_unused = st(1); esink_h = st(8); esink_s = st(8); sink_t = st(8)
        ss1 = st(4); l1 = st(4); rs1 = st(4)
        ssqk = st(40); lqk = st(40); rqk = st(40)
        den = st(16); rden = st(16)
        ssA = st(4); lA = st(4); rA = st(4)
        lC = st(8); rC = st(8)
        bmax = st(1)
        cwt = st(12); gCt = st(4)
        mark = A.off

        g1t = A.alloc(F32, 1024, ["g1t"])
        xn = [A.alloc(F32, 1024, ["xn%d" % i]) for i in range(2)]
        hb = [A.alloc(BF16, 1024, ["hb%d" % i]) for i in range(NHB)]
        Csb = A.alloc(F32, 512, ["Csb"]); ubuf = A.alloc(F32, 516, ["ubuf"]); acc = A.alloc(F32, 512, ["acc"]); yc = A.alloc(F32, 512, ["yc"])
        sqc = A.alloc(BF16, 512, ["sqc"])
        carry = A.alloc(F32, 8, ["carry%d" % i for i in range(4)]).rearrange("p (c n) -> p c n", c=4)
        sq32 = A.alloc(F32, 640, ["sq32a", "sq32b"]); tmp32 = A.alloc(F32, 640, ["tmp32a", "tmp32b"])
        qkn = [A.alloc(BF16, 768, ["qknq%d" % i, "qknk%d" % i]) for i in range(2)]
        gqk = A.alloc(F32, 768, ["gqk"])
        gq_t = A.alloc(F32, 64, ["gq_t"]); gk_t = A.alloc(F32, 64, ["gk_t"]); prod = A.alloc(F32, 64, ["prod"]); prod2 = A.alloc(F32, 64, ["prod2"])
        gAt = A.alloc(F32, 512, ["gAt"])
        maskg = A.alloc(BF16, 256, ["maskg"]); mask0 = A.alloc(BF16, 256, ["mask0"])
        kTr = A.alloc(BF16, 2 * 8 * 128, ["kT%d" % i for i in range(8)]).rearrange("p (k s n) -> p k s n", k=2, s=8)
        vr = A.alloc(BF16, 8 * 2 * 66, ["vr%d" % i for i in range(8)] + ["vr_ones"]).rearrange("p (s k d) -> p s k d", s=8, k=2)
        pT_l = [A.alloc(BF16, 2048, ["pT%d_%d" % (k, i) for i in range(4)]) for k in range(NPT)]
        yattn_l = [A.alloc(F32, 512, ["yattn%d_0" % k, "yattn%d_1" % k]) for k in range(NPT)]
        ya1 = A.alloc(BF16, 512, ["ya"])
        ya = [ya1, ya1]
        yT = A.alloc(BF16, 2048, ["yT%d" % i for i in range(4)]).rearrange("p (c n) -> p c n", c=4)
        ycT = A.alloc(BF16, 2048, ["ycT%d" % i for i in range(4)]).rearrange("p (c n) -> p c n", c=4)
        xr = [A.alloc(F32, 1024, ["xr%d" % i]) for i in range(2)]
        x1t = [A.alloc(F32, 1024, ["x1t%d_0" % i, "x1t%d_1" % i]) for i in range(NX1)]
        recsA = list(A.recs)
        endA = A.off

        import os as _os
        pools = {"conv": [0, 1, 2, 3, 4, 5, 6], "main": [0, 1, 2, 3, 4, 5, 6], "all": [0, 1, 2, 3, 4, 5, 6, 7]}
        if _os.environ.get("KPOOLS"):
            for part in _os.environ["KPOOLS"].split(";"):
                k, v = part.split(":")
                pools[k] = [int(t) for t in v.split(",")]
        if pools["conv"] == pools["main"]:
            bank_shared = True
        else:
            bank_shared = False
        bank_rr = {"conv": 0, "main": 0, "all": 0}

        def nb(pool="main"):
            if pool == "conv" and bank_shared:
                pool = "main"
            lst = pools[pool]
            b = lst[bank_rr[pool] % len(lst)]
            bank_rr[pool] += 1
            return b

        def fsz(ap):
            n = 1
            for d in ap.shape[1:]:
                n *= d
            return n

        def is_ps(ap):
            try:
                return ap.tensor.name.startswith("ps")
            except Exception:
                return False

        def dma(q, out_ap, in_ap, key, reads=(), writes=(), prio=0):
            nbytes = fsz(out_ap) * out_ap.shape[0] * 4
            if q == "sp":
                busy, lat = 120.0, 2500.0 + nbytes / 200.0
                if key.startswith("c_") and key not in ("c_id", "c_g1"):
                    lat = 22000.0
            else:
                busy, lat = 1500.0, 3500.0 + nbytes / 250.0
            P.add(q, lambda e: e.dma_start(out=out_ap, in_=in_ap), reads=reads, writes=writes, dma=key, busy=busy, lat=lat, prio=prio)

        def act(out_ap, in_ap, func, reads, writes, **kw):
            d = 230.0 + 0.83 * fsz(in_ap)
            P.add("act", lambda e: e.activation(out=out_ap, in_=in_ap, func=func, **kw), reads=reads, writes=writes, busy=d, lat=d + 60)

        def vdur(eng, out_ap, ins):
            n = fsz(out_ap)
            if eng == "dve":
                per = 1.04
                if out_ap.dtype == BF16 and all(a.dtype == BF16 for a in ins):
                    per = 0.55
                d = 110.0 + per * n + (70.0 if any(is_ps(a) for a in ins) else 0.0)
            else:
                d = 200.0 + 1.9 * n
            return d

        def copy(eng, out_ap, in_ap, reads, writes):
            if eng == "act":
                act(out_ap, in_ap, AF.Copy, reads, writes)
            else:
                d = vdur(eng, out_ap, [in_ap])
                P.add(eng, lambda e: e.tensor_copy(out=out_ap, in_=in_ap), reads=reads, writes=writes, busy=d, lat=d + 60)

        def tt(eng, out_ap, in0, in1, op, reads, writes):
            d = vdur(eng, out_ap, [in0, in1])
            P.add(eng, lambda e: e.tensor_tensor(out=out_ap, in0=in0, in1=in1, op=op), reads=reads, writes=writes, busy=d, lat=d + 60)

        def ts(eng, out_ap, in0, s1, op0, reads, writes, s2=None, op1=None):
            d = vdur(eng, out_ap, [in0])
            if op1 is None:
                P.add(eng, lambda e: e.tensor_scalar(out=out_ap, in0=in0, scalar1=s1, scalar2=None, op0=op0), reads=reads, writes=writes, busy=d, lat=d + 60)
            else:
                P.add(eng, lambda e: e.tensor_scalar(out=out_ap, in0=in0, scalar1=s1, scalar2=s2, op0=op0, op1=op1), reads=reads, writes=writes, busy=d, lat=d + 60)

        def stt(out_ap, in0, scalar, in1, op0, op1, reads, writes):
            d = vdur("dve", out_ap, [in0, in1])
            P.add("dve", lambda e: e.scalar_tensor_tensor(out=out_ap, in0=in0, scalar=scalar, in1=in1, op0=op0, op1=op1), reads=reads, writes=writes, busy=d, lat=d + 60)

        def mm(out_ap, lhsT, rhs, start, stop, reads, writes, **kw):
            n = fsz(rhs)
            if lhsT.dtype == F32:
                d = 135.0
            elif lhsT.shape[0] == 64:
                d = 200.0
            else:
                d = max(62.0, 16.0 + 0.405 * n)
            P.add("pe", lambda e: e.matmul(out_ap, lhsT=lhsT, rhs=rhs, start=start, stop=stop, **kw), reads=reads, writes=writes, busy=d, lat=d + 250)

        def tr(out_ap, in_ap, reads, writes):
            P.add("pe", lambda e: e.transpose(out=out_ap, in_=in_ap, identity=ident), reads=list(reads) + ["ident"], writes=writes, busy=90.0, lat=340.0)

        def rstd(ss_ap, l_ap, r_ap, n, names):
            act(l_ap, ss_ap, AF.Ln, [names[0]], [names[1]], scale=1.0 / n, bias=EPS)
            act(r_ap, l_ap, AF.Exp, [names[1]], [names[2]], scale=-0.5)

        def wload(dst, src, key, name, reads=()):
            dma("pool", dst, src, key, reads=reads, writes=[name])
        wload(Win[:, :, 0:768], w_in_v[:, :, 0:768], "wi0", "Win0")
        dma("sp", ident, ident_d, "c_id", writes=["ident"], prio=-1)
        dma("sp", g1t, g1.partition_broadcast(128), "c_g1", writes=["g1t"], prio=-1)
        dma("sp", gq_t, gq.partition_broadcast(128), "c_gq", writes=["gq_t"])
        dma("sp", gk_t, gk.partition_broadcast(128), "c_gk", writes=["gk_t"])
        dma("sp", sink_t, sinks.partition_broadcast(128), "c_sk", writes=["sink_t"])
        dma("sp", cwt, cw, "c_cw", writes=["cwt"])
        dma("sp", gCt, gC, "c_gC", writes=["gCt"])
        dma("sp", gAt, gA.partition_broadcast(128), "c_gA", writes=["gAt"])
        dma("sp", maskg, maskg_d, "c_mg", writes=["maskg"])
        dma("sp", mask0, mask0_d, "c_m0", writes=["mask0"])
        wload(Win[:, :, O3:O4], w_in_v[:, :, O3:O4], "wi2", "Win2")
        wload(Win[:, :, O4:2304], w_in_v[:, :, O4:2304], "wi3", "Win3")
        wload(Win[:, :, O2:O3], w_in_v[:, :, O2:O3], "wi1", "Win1")
        wload(Wout[:, :, :], w_out_v[:, :, :], "wo", "Wout")
        for j in range(8):
            wload(Wup[:, :, j * 512:(j + 1) * 512], w_up_v[:, :, j * 512:(j + 1) * 512], "wu%d" % j, "Wup%d" % j, reads=["x1s%d" % (1 + 4 * (j // 2))])

        P.add("pool", lambda e: e.memset(ones, 1.0), writes=["ones"])
        P.add("pool", lambda e: e.memset(vr[:, :, :, 64:65], 1.0), writes=["vr_ones"])
        cw3 = cwt.rearrange("p (c j) -> p c j", c=4)
        gqk3 = gqk.rearrange("p (h d) -> p h d", d=64)
        copy("dve", gqk3[:, 0:8, :], bc(gq_t, 1, 8), ["gq_t"], ["gqk"])
        ts("dve", gqk3[:, 8:12, :], bc(gk_t, 1, 4), 0.125, ALU.mult, ["gk_t"], ["gqk"])
        tt("dve", gqk3[:, 8:12, :], gqk3[:, 8:12, :], bc(gq_t, 1, 4), ALU.mult, ["gqk", "gq_t"], ["gqk"])
        tt("dve", prod, gq_t, gk_t, ALU.mult, ["gq_t", "gk_t"], ["prod"])
        ts("dve", prod2, prod, -1.0, ALU.mult, ["prod"], ["prod2"])
        tt("dve", prod, prod, prod2, ALU.max, ["prod", "prod2"], ["prod"])
        P.add("dve", lambda e: e.tensor_reduce(out=bmax, in_=prod, axis=AX.X, op=ALU.max), reads=["prod"], writes=["bmax"])
        ts("dve", negB, bmax, -8.0, ALU.mult, ["bmax"], ["negB"])
        act(esink_h, sink_t, AF.Exp, ["sink_t", "negB"], ["esink_h"], bias=negB, scale=1.0)
        copy("dve", esink_s.rearrange("p (k f c) -> p k f c", k=2, f=2), esink_h.rearrange("p (k c f) -> p k f c", k=2, c=2), ["esink_h"], ["esink_s"])

        def tiles_of(m):
            if m < 0:
                return [(0, 0)]
            return [(s, 1 + 4 * m + s) for s in range(4)]

        def NTs(m):
            P.tag = 'NT%d' % m
            for (s, ti) in tiles_of(m):
                sl = ti % 2; q4 = ti % 4; hs = ti % NHB
                col = s * 128
                dma("sp", xn[sl], xh[ti * 128:(ti + 1) * 128, :], "xn%d" % sl, writes=["xn%d" % sl], prio=-1)
                act(hb[hs], xn[sl], AF.Square, ["xn%d" % sl], ["hb%d" % hs, "ss1_%d" % q4], accum_out=ss1[:, q4:q4 + 1])
                rstd(ss1[:, q4:q4 + 1], l1[:, q4:q4 + 1], rs1[:, q4:q4 + 1], D, ("ss1_%d" % q4, "l1_%d" % q4, "rs1_%d" % q4))
                stt(hb[hs], xn[sl], rs1[:, q4:q4 + 1], g1t, ALU.mult, ALU.mult, ["xn%d" % sl, "rs1_%d" % q4, "g1t"], ["hb%d" % hs])
                b = nb()
                for c in range(8):
                    tr(ps16[b][:, c * 128:(c + 1) * 128], hb[hs][:, c * 128:(c + 1) * 128], ["hb%d" % hs], ["ps%d" % b])
                copy("act", hT[:, :, col:col + 128], ps16[b][:, :].rearrange("p (c n) -> p c n", c=8), ["ps%d" % b], ["hT%d" % s])

        def Zs(m):
            P.tag = 'Z%d' % m
            for (s, ti) in tiles_of(m):
                sl = ti % 2; q4 = ti % 4; s8 = ti % 8
                col = s * 128
                bq = nb()
                for kc in range(8):
                    mm(ps[bq][:, :], hT[:, kc, col:col + 128], Win[:, kc, 0:512], kc == 0, kc == 7, ["hT%d" % s, "Win0"], ["ps%d" % bq])
                bk = nb()
                for kc in range(8):
                    mm(ps[bk][:, 0:256], hT[:, kc, col:col + 128], Win[:, kc, 512:768], kc == 0, kc == 7, ["hT%d" % s, "Win0"], ["ps%d" % bk])
                act(sq32[:, 0:512], ps[bq][:, :], AF.Square, ["ps%d" % bq], ["sq32a"])
                act(sq32[:, 512:640], ps[bk][:, 0:128], AF.Square, ["ps%d" % bk], ["sq32b"])
                sv = ssqk[:, q4 * 10:(q4 + 1) * 10]; lv = lqk[:, q4 * 10:(q4 + 1) * 10]; rv = rqk[:, q4 * 10:(q4 + 1) * 10]
                P.add("dve", lambda e, sv=sv: e.tensor_reduce(out=sv, in_=sq32.rearrange("p (h d) -> p h d", d=64), axis=AX.X, op=ALU.add),
                      reads=["sq32a", "sq32b"], writes=["ssqk%d" % q4], busy=780.0, lat=840.0)
                rstd(sv, lv, rv, 64, ("ssqk%d" % q4, "lqk%d" % q4, "rqk%d" % q4))
                qn = qkn[sl]
                tt("dve", qn[:, 0:512].rearrange("p (h d) -> p h d", d=64), ps[bq][:, :].rearrange("p (h d) -> p h d", d=64),
                   bc(rv[:, 0:8], 2, 64), ALU.mult, ["ps%d" % bq, "rqk%d" % q4], ["qknq%d" % sl])
                tt("dve", tmp32[:, 512:640].rearrange("p (h d) -> p h d", d=64), ps[bk][:, 0:128].rearrange("p (h d) -> p h d", d=64),
                   bc(rv[:, 8:10], 2, 64), ALU.mult, ["ps%d" % bk, "rqk%d" % q4], ["tmp32b"])
                qn = qkn[sl]
                tt("pool", qn[:, 512:768].rearrange("p (k u d) -> p k u d", k=2, u=2),
                   bc(tmp32[:, 512:640].rearrange("p (k d) -> p k d", k=2), 2, 2),
                   gqk[:, 512:768].rearrange("p (k u d) -> p k u d", k=2, u=2), ALU.mult, ["tmp32b", "gqk"], ["qknk%d" % sl])
                copy("act", vr[:, s8, :, 0:64], ps[bk][:, 128:256].rearrange("p (k d) -> p k d", k=2), ["ps%d" % bk], ["vr%d" % s8])
                bt = nb()
                for c in range(6):
                    tr(ps16[bt][:, c * 128:(c + 1) * 128], qn[:, c * 128:(c + 1) * 128], ["qknq%d" % sl if c < 4 else "qknk%d" % sl], ["ps%d" % bt])
                copy("dve", qT[s], ps16[bt][:, 0:512].rearrange("p (c n) -> p c n", c=4), ["ps%d" % bt], ["qT%d" % s])
                copy("dve", kTr[:, :, s8, :], ps16[bt][:, 512:768].rearrange("p (k n) -> p k n", k=2), ["ps%d" % bt], ["kT%d" % s8])

        def ATTs(m):
            P.tag = 'ATT%d' % m
            for (s, ti) in tiles_of(m):
                sl = ti % 2; q4 = ti % 4
                col = s * 128
                kts = [(ti - 1) % 8, ti % 8]
                pi = ti % NPT
                pT = pT_l[pi]; yattn = yattn_l[pi]
                pT5 = pT.rearrange("p (j b c q) -> p j b c q", j=2, b=4, c=2)
                pTn = ["pT%d_%d" % (pi, b_) for b_ in range(4)]
                yan = ["yattn%d_%d" % (pi, k_) for k_ in range(2)]
                for kv in range(2):
                    bSs = [nb(), nb()]
                    for j in range(2):
                        for half in range(2):
                            bS = bSs[half]
                            lo = half * 64
                            mm(ps[bS][:, j * 256:(j + 1) * 256].rearrange("p (c n) -> p c n", c=2),
                               kTr[lo:lo + 64, kv, kts[j], :], qT[s][lo:lo + 64, 2 * kv:2 * kv + 2, :], True, True,
                               ["kT%d" % kts[j], "qT%d" % s], ["ps%d" % bS])
                    for half in range(2):
                        b4 = kv * 2 + half
                        bS = bSs[half]
                        act(pT5[:, :, b4, :, :], ps[bS][:, :].rearrange("p (j c q) -> p j c q", j=2, c=2), AF.Exp,
                            ["ps%d" % bS, "negB"], [pTn[b4]], bias=negB, scale=1.0)
                mk = mask0 if ti == 1 else maskg
                mkn = "mask0" if ti == 1 else "maskg"
                pT4 = pT.rearrange("p (j r q) -> p j r q", j=2, r=8)
                for kv in range(2):
                    pv = pT4[:, :, 4 * kv:4 * kv + 4, :]
                    tt(MASK_ENG, pv, pv, bc(mk.rearrange("p (j q) -> p j q", j=2), 2, 4), ALU.mult,
                       [pTn[2 * kv], pTn[2 * kv + 1], mkn], [pTn[2 * kv], pTn[2 * kv + 1]])
                d8 = den[:, sl * 8:(sl + 1) * 8]; r8 = rden[:, sl * 8:(sl + 1) * 8]
                bOs = []
                for kv in range(2):
                    bO = nb(); bOs.append(bO)
                    for half in range(2):
                        for c in range(2):
                            s4 = half * 2 + c; b4 = kv * 2 + half
                            for j in range(2):
                                mm(ps[bO][:, s4 * 128:s4 * 128 + 65], pT5[:, j, b4, c, :], vr[:, kts[j], kv, 0:65], j == 0, j == 1,
                                   [pTn[b4], "vr%d" % kts[j], "vr_ones"], ["ps%d" % bO])
                    o3v = ps[bO][:, :].rearrange("p (s x) -> p s x", s=4)
                    tt("dve", d8[:, kv * 4:kv * 4 + 4], o3v[:, :, 64], esink_s[:, kv * 4:kv * 4 + 4], ALU.add, ["ps%d" % bO, "esink_s"], ["den%d_%d" % (sl, kv)])
                P.add("dve", lambda e, d8=d8, r8=r8: e.reciprocal(out=r8, in_=d8), reads=["den%d_0" % sl, "den%d_1" % sl], writes=["rden%d" % sl])
                for kv in range(2):
                    bO = bOs[kv]
                    tt("dve", yattn[:, kv * 256:(kv + 1) * 256].rearrange("p (c f d) -> p f c d", c=2, f=2),
                       ps[bO][:, :].rearrange("p (f c x) -> p f c x", f=2, c=2)[:, :, :, 0:64],
                       bc(r8[:, kv * 4:kv * 4 + 4].rearrange("p (f c) -> p f c", f=2), 3, 64), ALU.mult,
                       ["ps%d" % bO, "rden%d" % sl], [yan[kv]])
                act(tmp32[:, 0:512], yattn, AF.Square, yan, ["tmp32a", "ssA%d" % q4], accum_out=ssA[:, q4:q4 + 1])
                rstd(ssA[:, q4:q4 + 1], lA[:, q4:q4 + 1], rA[:, q4:q4 + 1], 512, ("ssA%d" % q4, "lA%d" % q4, "rA%d" % q4))
                stt(ya[sl], yattn, rA[:, q4:q4 + 1], gAt, ALU.mult, ALU.mult, yan + ["rA%d" % q4, "gAt"], ["ya"])
                bt = nb()
                for c in range(4):
                    tr(ps16[bt][:, c * 128:(c + 1) * 128], ya[sl][:, c * 128:(c + 1) * 128], ["ya"], ["ps%d" % bt])
                copy("act", yT[:, :, col:col + 128], ps16[bt][:, 0:512].rearrange("p (c n) -> p c n", c=4), ["ps%d" % bt], ["yT%d" % s])

        def CONVs(m):
            P.tag = 'CONV%d' % m
            halo = m < 0
            N = 128 if halo else 512
            hts = ["hT0"] if halo else ["hT0", "hT1", "hT2", "hT3"]
            m2 = m % 2
            for i in range(4):
                def grp(coff, wname):
                    b = nb("conv")
                    for kc in range(8):
                        mm(ps[b][:, 0:N], Win[:, kc, coff + 128 * i: coff + 128 * (i + 1)], hT[:, kc, 0:N], kc == 0, kc == 7, hts + [wname], ["ps%d" % b])
                    return b
                bC = grp(O3, "Win2")
                bX = grp(O4, "Win3")
                copy("act", Csb[:, 0:N], ps[bC][:, 0:N], ["ps%d" % bC], ["Csb"])
                if not halo:
                    copy("pool", ubuf[:, 0:2], carry[:, i, :], ["carry%d" % i], ["ubuf"])
                tt("dve", ubuf[:, 2:2 + N], Csb[:, 0:N], ps[bX][:, 0:N], ALU.mult, ["Csb", "ps%d" % bX], ["ubuf"])
                copy("pool", carry[:, i, :], ubuf[:, N:N + 2], ["ubuf"], ["carry%d" % i])
                if halo:
                    continue
                bB = grp(O2, "Win1")
                ts("dve", acc, ubuf[:, 2:514], cw3[:, i, 2:3], ALU.mult, ["ubuf", "cwt"], ["acc"])
                stt(acc, ubuf[:, 1:513], cw3[:, i, 1:2], acc, ALU.mult, ALU.add, ["ubuf", "cwt", "acc"], ["acc"])
                stt(acc, ubuf[:, 0:512], cw3[:, i, 0:1], acc, ALU.mult, ALU.add, ["ubuf", "cwt", "acc"], ["acc"])
                tt("dve", yc, acc, ps[bB][:, :], ALU.mult, ["acc", "ps%d" % bB], ["yc"])
                act(sqc, yc, AF.Square, ["yc"], ["sqc"])
                for s in range(4):
                    mm(ps[7][:, s:s + 1], sqc[:, s * 128:(s + 1) * 128], ones, (i == 0 and s == 0), (i == 3 and s == 3), ["sqc", "ones"], ["ps7"], skip_group_check=True)
                act(ycT[:, i, :], yc, AF.Identity, ["yc", "gCt"], ["ycT%d" % i], scale=gCt[:, i:i + 1])
            if not halo:
                lv = lC[:, m2 * 4:m2 * 4 + 4]; rv = rC[:, m2 * 4:m2 * 4 + 4]
                act(lv, ps[7][:, 0:4], AF.Ln, ["ps7"], ["lC%d" % m2], scale=1.0 / 512, bias=EPS)
                act(rv, lv, AF.Exp, ["lC%d" % m2], ["rC%d" % m2], scale=-0.5)

        def OUTs(m):
            P.tag = 'OUT%d' % m
            m2 = m % 2
            for (s, ti) in tiles_of(m):
                sl = ti % 2
                col = s * 128
                dma("sp", xr[sl], xh[ti * 128:(ti + 1) * 128, :], "xr%d" % sl, writes=["xr%d" % sl])
                for n in range(2):
                    bA = nb()
                    for c in range(4):
                        mm(ps[bA][:, :], yT[:, c, col:col + 128], Wout[:, c, n * 512:(n + 1) * 512], c == 0, c == 3, ["yT%d" % s, "Wout"], ["ps%d" % bA])
                    bCc = nb()
                    for c in range(4):
                        mm(ps[bCc][:, :], ycT[:, c, col:col + 128], Wout[:, 4 + c, n * 512:(n + 1) * 512], c == 0, c == 3, ["ycT%d" % c, "Wout"], ["ps%d" % bCc])
                    x1 = ti % NX1
                    xs = x1t[x1][:, n * 512:(n + 1) * 512]
                    stt(xs, ps[bCc][:, :], rC[:, m2 * 4 + s:m2 * 4 + s + 1], xr[sl][:, n * 512:(n + 1) * 512], ALU.mult, ALU.add,
                        ["ps%d" % bCc, "rC%d" % m2, "xr%d" % sl], ["x1t%d_%d" % (x1, n)])
                    tt("dve", xs, xs, ps[bA][:, :], ALU.add, ["x1t%d_%d" % (x1, n), "ps%d" % bA], ["x1t%d_%d" % (x1, n)])
                x1 = ti % NX1
                dma("sp", x1s[(ti - 1) * 128:ti * 128, :], x1t[x1], "x1st%d" % x1, reads=["x1t%d_0" % x1, "x1t%d_1" % x1], writes=["x1s%d" % ti])

        NTs(-1); Zs(-1); CONVs(-1)
        NTs(0)
        for m in range(NM):
            Zs(m)
            CONVs(m)
            ATTs(m)
            if m + 1 < NM:
                NTs(m + 1)
            OUTs(m)

        A.off = mark
        nA = len(A.recs)
        g2t = A.alloc(F32, 1024, ["g2t"])
        xn2 = [A.alloc(F32, 1024, ["xn2_%d" % i]) for i in range(2)]
        hm = [A.alloc(BF16, 1024, ["hm%d" % i]) for i in range(2)]
        hmT = [None, None]
        hmT[0] = A.alloc(BF16, 4096, ["hmT0_%d" % i for i in range(4)]).rearrange("p (c n) -> p c n", c=8)
        r32 = [A.alloc(F32, 512, ["r32_%d" % i]) for i in range(2)]
        hmT[1] = A.alloc(BF16, 4096, ["hmT1_%d" % i for i in range(4)]).rearrange("p (c n) -> p c n", c=8)
        xr2 = [A.alloc(F32, 1024, ["xr2_%d_0" % i, "xr2_%d_1" % i]) for i in range(2)]
        actT = A.alloc(BF16, 32 * 512, ["actT%d" % i for i in range(32)]).rearrange("p (f n) -> p f n", f=32)
        ss2 = st(4); l2 = st(4); rs2 = st(4)
        for (b0, b1, bn) in A.recs[nA:]:
            for (a0, a1, an) in recsA:
                if a0 < b1 and b0 < a1:
                    for nm in bn:
                        P.alias.setdefault(nm, []).extend(an)
        wa_in = ["Win0", "Win1", "Win2", "Win3"]
        wa_out = ["Wout"]
        wa_sp = ["hT%d" % i for i in range(4)] + ["qT%d" % i for i in range(4)]
        P.alias["Wd0"] = list(wa_in); P.alias["Wd1"] = list(wa_in); P.alias["Wd2"] = wa_in + wa_out; P.alias["Wd3"] = wa_out + wa_sp

        for j in range(4):
            dma("pool", Wd[:, 8 * j:8 * j + 8, :], w_down_v[:, 8 * j:8 * j + 8, :], "wd%d" % j, writes=["Wd%d" % j])
        dma("sp", g2t, g2.partition_broadcast(128), "c_g2", writes=["g2t"])

        def NT2(m):
            P.tag = 'NT2%d' % m
            mb = m % 2
            for (s, ti) in tiles_of(m):
                sl = ti % 2; q4 = ti % 4
                col = s * 128
                dma("sp", xn2[sl], x1s[(ti - 1) * 128:ti * 128, :], "xn2_%d" % sl, reads=["x1s%d" % ti], writes=["xn2_%d" % sl])
                act(hm[sl], xn2[sl], AF.Square, ["xn2_%d" % sl], ["hm%d" % sl, "ss2_%d" % q4], accum_out=ss2[:, q4:q4 + 1])
                rstd(ss2[:, q4:q4 + 1], l2[:, q4:q4 + 1], rs2[:, q4:q4 + 1], D, ("ss2_%d" % q4, "l2_%d" % q4, "rs2_%d" % q4))
                stt(hm[sl], xn2[sl], rs2[:, q4:q4 + 1], g2t, ALU.mult, ALU.mult, ["xn2_%d" % sl, "rs2_%d" % q4, "g2t"], ["hm%d" % sl])
                b = nb("all")
                for c in range(8):
                    tr(ps16[b][:, c * 128:(c + 1) * 128], hm[sl][:, c * 128:(c + 1) * 128], ["hm%d" % sl], ["ps%d" % b])
                copy("act", hmT[mb][:, :, col:col + 128], ps16[b][:, :].rearrange("p (c n) -> p c n", c=8), ["ps%d" % b], ["hmT%d_%d" % (mb, s)])

        def UP(m):
            P.tag = 'UP%d' % m
            mb = m % 2
            hn = ["hmT%d_%d" % (mb, s) for s in range(4)]
            for f in range(32):
                b = nb("all")
                for kc in range(8):
                    mm(ps[b][:, :], Wup[:, kc, f * 128:(f + 1) * 128], hmT[mb][:, kc, :], kc == 0, kc == 7, hn + ["Wup%d" % (f // 4)], ["ps%d" % b])
                r = r32[f % 2]
                act(r, ps[b][:, :], AF.Relu, ["ps%d" % b], ["r32_%d" % (f % 2)])
                tt("dve" if f % 2 == 0 else "pool", actT[:, f, :], r, r, ALU.mult, ["r32_%d" % (f % 2)], ["actT%d" % f])

        def DOWN(m):
            P.tag = 'DOWN%d' % m
            for (s, ti) in tiles_of(m):
                sl = ti % 2
                col = s * 128
                dma("sp", xr2[sl], x1s[(ti - 1) * 128:ti * 128, :], "xr2_%d" % sl, reads=["x1s%d" % ti], writes=["xr2_%d_0" % sl, "xr2_%d_1" % sl])
                for n in range(2):
                    b = nb("all")
                    for f in range(32):
                        mm(ps[b][:, :], actT[:, f, col:col + 128], Wd[:, f, n * 512:(n + 1) * 512], f == 0, f == 31, ["actT%d" % f, "Wd%d" % (f // 8)], ["ps%d" % b])
                    xs = xr2[sl][:, n * 512:(n + 1) * 512]
                    tt("dve", xs, xs, ps[b][:, :], ALU.add, ["xr2_%d_%d" % (sl, n), "ps%d" % b], ["xr2_%d_%d" % (sl, n)])
                dma("sp", out[(ti - 1) * 128:ti * 128, :], xr2[sl], "ost%d" % sl, reads=["xr2_%d_0" % sl, "xr2_%d_1" % sl])

        NT2(0)
        for m in range(NM):
            UP(m)
            if m + 1 < NM:
                NT2(m + 1)
            DOWN(m)

        P.finalize()
        sems = {}
        for k in P.dma_counts:
            sems[("dma", k)] = es.enter_context(nc.semaphore("d_" + k))
        for k in ENGS:
            sems[("eng", k)] = es.enter_context(nc.semaphore("e_" + k))
        with nc.Block() as block:
            @block.sync
            def _(e):
                P.emit("sp", e, sems, final_waits=[("dma", "ost0"), ("dma", "ost1")])

            @block.gpsimd
            def _(e):
                P.emit("pool", e, sems)

            @block.scalar
            def _(e):
                P.emit("act", e, sems)

            @block.vector
            def _(e):
                P.emit("dve", e, sems)

            @block.tensor
            def _(e):
                P.emit("pe", e, sems)
    return nc


_NC_CACHE = {}


def prep_inputs(x, attn_norm_g, w_in, q_norm_g, k_norm_g, sinks, conv_w, attn_out_g, conv_out_g, w_out, mlp_norm_g, w_up, w_down):
    f = lambda a: np.ascontiguousarray(np.asarray(a, dtype=np.float32))
    x = f(x)
    B, S, _ = x.shape
    assert (B, S) == (4, 8192)
    bf = ml_dtypes.bfloat16
    kk = np.arange(128)[:, None]; qq = np.arange(128)[None, :]
    m_prev = (kk > qq).astype(np.float32); m_cur = (kk <= qq).astype(np.float32)
    maskg = np.concatenate([m_prev, m_cur], axis=1).astype(bf)
    mask0_first = np.concatenate([np.zeros_like(m_prev), m_cur], axis=1).astype(bf)
    ident = np.eye(128, dtype=np.float32).astype(bf)
    common = dict(
        w_in=f(w_in[0]), w_out=f(w_out[0]), w_up=f(w_up[0]), w_down=f(w_down[0]),
        g1=f(attn_norm_g[0]).reshape(1, D), g2=f(mlp_norm_g[0]).reshape(1, D),
        gq=f(q_norm_g[0]).reshape(1, 64), gk=f(k_norm_g[0]).reshape(1, 64), sinks=f(sinks[0]).reshape(1, 8),
        cw=np.ascontiguousarray(f(conv_w[0]).reshape(3, 4, 128).transpose(2, 1, 0).reshape(128, 12)),
        gA=f(attn_out_g[0]).reshape(1, 512),
        gC=np.ascontiguousarray(f(conv_out_g[0]).reshape(4, 128).T),
        ident=ident, maskg=maskg,
    )
    in_maps = []
    for c in range(8):
        b, half = c // 2, c % 2
        if half == 0:
            xh = np.concatenate([np.zeros((128, D), np.float32), x[b, 0:T]], axis=0)
            m0 = mask0_first
        else:
            xh = x[b, T - 128:2 * T]
            m0 = maskg
        d = dict(common)
        d["xh"] = np.ascontiguousarray(xh)
        d["mask0"] = m0
        in_maps.append(d)
    return in_maps


def kernel(**inputs):
    in_maps = prep_inputs(**inputs)
    B, S = 4, 8192
    if "nc" not in _NC_CACHE:
        _NC_CACHE["nc"] = build_nc()
    nc = _NC_CACHE["nc"]
    res = run_bass_kernel_spmd(nc, in_maps, core_ids=list(range(8)))
    outp = np.empty((B, S, D), np.float32)
    for c in range(8):
        b, half = c // 2, c % 2
        outp[b, half * T:(half + 1) * T] = res.results[c]["out"]
    return outp
```
